# Optimizing a Trainium2 kernel written in Bass

```python
import math
import jax, jax.numpy as jnp
from jax import lax
import numpy as np

D_MODEL = 1024
BATCH = 8
SEQ = 4096
DEPTH = 4

GRID_W = 64
CTX_LEN = 256
D_MIX = D_MODEL
CONV_W = D_MIX // 4
CONV_GROUPS = 4
CONV_K = 31
GLA_W = D_MIX // 4
GLA_HEADS = 4
GLA_DV = GLA_W // GLA_HEADS
GLA_DK = GLA_DV // 2
GLA_RANK = 16
GLA_GATE_NORM = 16.0
GLA_CHUNK = 64
ATT_W = D_MIX - CONV_W - GLA_W
HEAD_DIM = 64
N_HEADS = ATT_W // HEAD_DIM
N_KV_HEADS = 2
Q_BLOCK = 128
ROPE_BASE = 10000.0
ROT_FREQS = HEAD_DIM // 4
D_FF = 2816
N_MOD = 9
EPS = 1e-6
IN_SIZES = (CONV_W, CONV_W, GLA_HEADS * GLA_DK, GLA_HEADS * GLA_DK, GLA_W, GLA_W,
            GLA_RANK, GLA_RANK, ATT_W, N_KV_HEADS * HEAD_DIM, N_KV_HEADS * HEAD_DIM)
IN_WIDTH = sum(IN_SIZES)

kernel_name = "hymba_style_conv_gla_gqa_macaron_dit"


def rms_norm(x, g):
    xf = x.astype(jnp.float32)
    y = xf * lax.rsqrt(jnp.mean(xf * xf, axis=-1, keepdims=True) + EPS)
    return (y * g.astype(jnp.float32)).astype(x.dtype)


def pre(x, mod, j, g):
    return rms_norm(x, g) * (1 + mod[:, :, 3 * j + 1]) + mod[:, :, 3 * j]


def half_ffn(x, mod, j, g, w1, w2):
    gu = pre(x, mod, j, g) @ w1
    a, u = jnp.split(gu, 2, axis=-1)
    return x + 0.5 * mod[:, :, 3 * j + 2] * ((jax.nn.silu(a) * u) @ w2)


def split_proj(p):
    idx = np.cumsum(np.array(IN_SIZES))[:-1].tolist()
    return jnp.split(p, idx, axis=-1)


def conv_module(a, gate, w_dw, b_dw, g, b):
    h = a * jax.nn.sigmoid(gate)
    h = lax.conv_general_dilated(h, w_dw[:, None, :].astype(h.dtype), window_strides=(1,),
                                 padding=[(CONV_K // 2, CONV_K // 2)],
                                 dimension_numbers=('NWC', 'WIO', 'NWC'),
                                 feature_group_count=CONV_W) + b_dw
    B, T, C = h.shape
    hf = h.astype(jnp.float32).reshape(B, T, CONV_GROUPS, C // CONV_GROUPS)
    mu = jnp.mean(hf, axis=-1, keepdims=True)
    var = jnp.mean(jnp.square(hf - mu), axis=-1, keepdims=True)
    hf = ((hf - mu) * lax.rsqrt(var + EPS)).reshape(B, T, C) * g + b
    return jax.nn.silu(hf).astype(a.dtype)


def gla_inputs(p, w_gate, b_gate):
    B, T, _ = p[2].shape
    f = lambda t, d: t.astype(jnp.float32).reshape(B, T, GLA_HEADS, d)
    q = f(p[2], GLA_DK) * (GLA_DK ** -0.5)
    k = f(p[3], GLA_DK)
    v = f(p[4], GLA_DV)
    la_f = f(jax.nn.log_sigmoid(p[6].astype(jnp.float32) @ w_gate[0].astype(jnp.float32)
                                 + b_gate[0].astype(jnp.float32)) / GLA_GATE_NORM, GLA_DK)
    la_b = f(jax.nn.log_sigmoid(p[7].astype(jnp.float32) @ w_gate[1].astype(jnp.float32)
                                 + b_gate[1].astype(jnp.float32)) / GLA_GATE_NORM, GLA_DK)
    return q, k, v, la_f, la_b


def gla_chunked(q, k, v, log_a, s0):
    B, T, H, DK = q.shape
    DV = v.shape[-1]
    C = GLA_CHUNK
    N = T // C
    q = q.reshape(B, N, C, H, DK)
    k = k.reshape(B, N, C, H, DK)
    v = v.reshape(B, N, C, H, DV)
    bcum = jnp.cumsum(log_a.reshape(B, N, C, H, DK), axis=2)
    b_last = bcum[:, :, -1]
    q_e = q * jnp.exp(bcum)
    k_e = k * jnp.exp(-bcum)
    k_end = k * jnp.exp(b_last[:, :, None] - bcum)
    mask = jnp.tril(jnp.ones((C, C), dtype=bool))
    att = jnp.where(mask, jnp.einsum('bnihd,bnjhd->bnhij', q_e, k_e), 0.0)
    o = jnp.einsum('bnhij,bnjhv->bnihv', att, v)
    ds = jnp.einsum('bnjhd,bnjhv->bnhdv', k_end, v)
    decay = jnp.exp(b_last)

    def step(s, inp):
        dec, d = inp
        return dec[..., None] * s + d, s

    s_fin, s_start = lax.scan(step, s0, (jnp.moveaxis(decay, 1, 0), jnp.moveaxis(ds, 1, 0)))
    s_start = jnp.moveaxis(s_start, 0, 1)
    o = o + jnp.einsum('bnihd,bnhdv->bnihv', q_e, s_start)
    return o.reshape(B, T, H, DV), s_fin


def gla_bidir(q, k, v, la_f, la_b, s0_f, s0_b):
    flip = lambda t: jnp.flip(t, axis=1)
    o_f, s_f = gla_chunked(q, k, v, la_f, s0_f)
    o_b, s_b = gla_chunked(flip(q), flip(k), flip(v), flip(la_b), s0_b)
    return o_f + flip(o_b), s_f, s_b


def gla_output(o, r, g):
    B, T = r.shape[:2]
    on = rms_norm(o, g).reshape(B, T, GLA_W)
    return (on * jax.nn.silu(r.astype(jnp.float32))).astype(r.dtype)


def rope2d(x, cos, sin):
    B, T, H, HD = x.shape
    xr = x.reshape(B, T, H, 2, 2, HD // 4)
    x1, x2 = xr[..., 0, :], xr[..., 1, :]
    c, s = cos[:, None], sin[:, None]
    out = jnp.stack([x1 * c - x2 * s, x2 * c + x1 * s], axis=-2)
    return out.reshape(B, T, H, HD).astype(x.dtype)


def group_q(q):
    B, T, H, HD = q.shape
    return (q * (HD ** -0.5)).reshape(B, T, N_KV_HEADS, H // N_KV_HEADS, HD)


def attend(qi, k, v):
    s = jnp.einsum('bqkgd,bskd->bkgqs', qi, k).astype(jnp.float32)
    p = jax.nn.softmax(s, axis=-1).astype(v.dtype)
    return jnp.einsum('bkgqs,bskd->bqkgd', p, v)


def attention_latent(q, k_all, v_all):
    B, S = q.shape[:2]
    nb = S // Q_BLOCK
    qg = group_q(q)
    qb = jnp.moveaxis(qg.reshape(B, nb, Q_BLOCK, *qg.shape[2:]), 1, 0)
    o = lax.map(lambda qi: attend(qi, k_all, v_all), qb)
    return jnp.moveaxis(o, 0, 1).reshape(B, S, ATT_W)


def token_mix(hx, hc, cos, sin, w_in, w_out, w_dw, b_dw, cn_g, cn_b, w_gate, b_gate,
              gla_g, qn_g, kn_g, last):
    B, S, _ = hx.shape
    L = hc.shape[1]
    px = split_proj(hx @ w_in)
    pc = split_proj(hc @ w_in)
    conv_x = conv_module(px[0], px[1], w_dw, b_dw, cn_g, cn_b)
    qx, kx, vx, lfx, lbx = gla_inputs(px, w_gate, b_gate)
    qc, kc, vc, lfc, lbc = gla_inputs(pc, w_gate, b_gate)
    zeros = jnp.zeros((B, GLA_HEADS, GLA_DK, GLA_DV), jnp.float32)
    o_c, s_f, s_b = gla_bidir(qc, kc, vc, lfc, lbc, zeros, zeros)
    o_x, _, _ = gla_bidir(qx, kx, vx, lfx, lbx, s_f, s_b)
    gla_x = gla_output(o_x, px[5], gla_g)
    aqx = rope2d(rms_norm(px[8].reshape(B, S, N_HEADS, HEAD_DIM), qn_g), cos, sin)
    akx = rope2d(rms_norm(px[9].reshape(B, S, N_KV_HEADS, HEAD_DIM), kn_g), cos, sin)
    avx = px[10].reshape(B, S, N_KV_HEADS, HEAD_DIM)
    akc = rms_norm(pc[9].reshape(B, L, N_KV_HEADS, HEAD_DIM), kn_g)
    avc = pc[10].reshape(B, L, N_KV_HEADS, HEAD_DIM)
    k_all = jnp.concatenate([akc, akx], axis=1)
    v_all = jnp.concatenate([avc, avx], axis=1)
    att_x = attention_latent(aqx, k_all, v_all)
    y_x = jnp.concatenate([conv_x, gla_x, att_x], axis=-1) @ w_out
    if last:
        return y_x, None
    conv_c = conv_module(pc[0], pc[1], w_dw, b_dw, cn_g, cn_b)
    gla_c = gla_output(o_c, pc[5], gla_g)
    aqc = rms_norm(pc[8].reshape(B, L, N_HEADS, HEAD_DIM), qn_g)
    att_c = attend(group_q(aqc), akc, avc).reshape(B, L, ATT_W)
    y_c = jnp.concatenate([conv_c, gla_c, att_c], axis=-1) @ w_out
    return y_x, y_c


def layer(x, ctx, mod_x, mod_c, cos, sin, g_norm, w_ffn_in, w_ffn_out, w_in, w_out, w_dw, b_dw,
          cn_g, cn_b, w_gate, b_gate, gla_g, qn_g, kn_g, last):
    x = half_ffn(x, mod_x, 0, g_norm[0], w_ffn_in[0], w_ffn_out[0])
    ctx = half_ffn(ctx, mod_c, 0, g_norm[0], w_ffn_in[0], w_ffn_out[0])
    hx = pre(x, mod_x, 1, g_norm[1])
    hc = pre(ctx, mod_c, 1, g_norm[1])
    y_x, y_c = token_mix(hx, hc, cos, sin, w_in, w_out, w_dw, b_dw, cn_g, cn_b, w_gate, b_gate,
                         gla_g, qn_g, kn_g, last)
    x = x + mod_x[:, :, 5] * y_x
    x = half_ffn(x, mod_x, 2, g_norm[2], w_ffn_in[1], w_ffn_out[1])
    if not last:
        ctx = ctx + mod_c[:, :, 5] * y_c
        ctx = half_ffn(ctx, mod_c, 2, g_norm[2], w_ffn_in[1], w_ffn_out[1])
    return x, ctx


def setup_inputs(seed: int = 0) -> dict:
    key = jax.random.key(seed)
    ks = jax.random.split(key, 20)
    nrm = lambda k, shape, s: jax.random.normal(k, shape, jnp.float32) * s
    D = D_MODEL
    return {
        "x": nrm(ks[0], (BATCH, SEQ, D), 1.0),
        "c": nrm(ks[1], (BATCH, D), 1.0),
        "ctx": nrm(ks[2], (BATCH, CTX_LEN, D), 1.0),
        "c_ctx": nrm(ks[3], (D,), 1.0),
        "w_ada": nrm(ks[4], (DEPTH, D, N_MOD * D), 0.5 * D ** -0.5),
        "b_ada": nrm(ks[5], (DEPTH, N_MOD * D), 0.02),
        "g_norm": 1.0 + nrm(ks[6], (DEPTH, 3, D), 0.02),
        "w_ffn_in": nrm(ks[7], (DEPTH, 2, D, 2 * D_FF), D ** -0.5),
        "w_ffn_out": nrm(ks[8], (DEPTH, 2, D_FF, D), D_FF ** -0.5),
        "w_in": nrm(ks[9], (DEPTH, D, IN_WIDTH), D ** -0.5),
        "w_out": nrm(ks[10], (DEPTH, D_MIX, D), D_MIX ** -0.5),
        "w_dw": nrm(ks[11], (DEPTH, CONV_K, CONV_W), CONV_K ** -0.5),
        "b_dw": nrm(ks[12], (DEPTH, CONV_W), 0.02),
        "conv_norm_g": 1.0 + nrm(ks[13], (DEPTH, CONV_W), 0.02),
        "conv_norm_b": nrm(ks[14], (DEPTH, CONV_W), 0.02),
        "w_gla_gate": nrm(ks[15], (DEPTH, 2, GLA_RANK, GLA_HEADS * GLA_DK), GLA_RANK ** -0.5),
        "b_gla_gate": nrm(ks[16], (DEPTH, 2, GLA_HEADS * GLA_DK), 0.1),
        "gla_norm_g": 1.0 + nrm(ks[17], (DEPTH, GLA_HEADS, GLA_DV), 0.02),
        "q_norm_g": 1.0 + nrm(ks[18], (DEPTH, HEAD_DIM), 0.02),
        "k_norm_g": 1.0 + nrm(ks[19], (DEPTH, HEAD_DIM), 0.02),
    }


def reference(x, c, ctx, c_ctx, w_ada, b_ada, g_norm, w_ffn_in, w_ffn_out, w_in, w_out, w_dw, b_dw,
              conv_norm_g, conv_norm_b, w_gla_gate, b_gla_gate, gla_norm_g, q_norm_g, k_norm_g):
    B, S, D = x.shape
    rows_n = S // GRID_W
    row = jnp.repeat(jnp.arange(rows_n), GRID_W).astype(jnp.float32)
    col = jnp.tile(jnp.arange(GRID_W), rows_n).astype(jnp.float32)
    freqs = ROPE_BASE ** (-jnp.arange(ROT_FREQS, dtype=jnp.float32) / ROT_FREQS)
    ang = jnp.stack([row[:, None] * freqs, col[:, None] * freqs], axis=1)
    cos, sin = jnp.cos(ang), jnp.sin(ang)
    sc = jax.nn.silu(c)
    scc = jax.nn.silu(c_ctx)
    for i in range(DEPTH):
        last = i == DEPTH - 1
        mod_x = (sc @ w_ada[i] + b_ada[i]).reshape(B, 1, N_MOD, D)
        mod_c = (scc @ w_ada[i] + b_ada[i]).reshape(1, 1, N_MOD, D)
        x, ctx = layer(x, ctx, mod_x, mod_c, cos, sin, g_norm[i], w_ffn_in[i], w_ffn_out[i],
                       w_in[i], w_out[i], w_dw[i], b_dw[i], conv_norm_g[i], conv_norm_b[i],
                       w_gla_gate[i], b_gla_gate[i], gla_norm_g[i], q_norm_g[i], k_norm_g[i], last)
    return x
```

```python
from contextlib import ExitStack
import math
import numpy as np
import concourse.bass as bass
import concourse.mybir as mybir
from concourse.bass_utils import run_bass_kernel_spmd

F32 = mybir.dt.float32
BF16 = mybir.dt.bfloat16
ALU = mybir.AluOpType
AF = mybir.ActivationFunctionType

D = 1024
SEQ = 4096
NCTX = 256
NT = SEQ + NCTX
DEPTH = 4
DFF = 2816
NJ = DFF // 128
INW = 2080
EPS = 1e-6
TILES = [(0, 256)] + [(256 + 512 * i, 512) for i in range(8)]
PCW = 15 + 256 + 15 + 15 + 4096 + 15


def pc_off(n):
    return 15 + n if n < 256 else n + 45


class Buf:
    __slots__ = ("name", "w", "r")

    def __init__(self, name):
        self.name = name
        self.w = None
        self.r = []


class Op:
    __slots__ = ("eng", "fn", "deps", "dma", "stream", "signal", "count", "sem", "epoch", "widx")


ENGS = ("pe", "act", "dve", "pool", "sp")
DMA_WIN = 6
POOL_WIN = False
ATT_RATIO = 1.75
ENGMAP = {"pe": "tensor", "act": "scalar", "dve": "vector", "pool": "gpsimd", "sp": "sync"}


class Prog:
    def __init__(self, nc):
        self.nc = nc
        self.ops = {e: [] for e in ENGS}
        self.all = []
        self.epoch = 0
        self.last = {}
        self.hist = {}

    def buf(self, name="b"):
        return Buf(name)

    def _mk(self, eng, fn, deps, dma, stream):
        op = Op()
        op.eng, op.fn, op.dma, op.stream = eng, fn, dma, stream
        op.signal = dma
        op.count = op.sem = None
        op.epoch = self.epoch
        op.widx = None
        if dma and (eng != "pool" or POOL_WIN):
            h = self.hist.setdefault(stream, [])
            op.widx = len(h)
            if len(h) >= DMA_WIN:
                deps = list(deps) + [h[len(h) - DMA_WIN]]
            h.append(op)
        seen = set()
        dd = []
        for d in deps:
            if id(d) in seen:
                continue
            seen.add(id(d))
            if (not dma) and (not d.dma) and d.eng == eng and eng == "pe":
                continue
            dd.append(d)
            d.signal = True
        op.deps = dd
        self.ops[eng].append(op)
        self.all.append(op)
        self.last[stream if dma else eng] = op
        return op

    def _add(self, eng, fn, reads, writes, dma=False, stream=None):
        deps = []
        for b in reads:
            if b.w is not None:
                deps.append(b.w)
        for b in writes:
            if b.w is not None:
                deps.append(b.w)
            deps.extend(b.r)
        op = self._mk(eng, fn, deps, dma, stream)
        for b in reads:
            b.r.append(op)
        for b in writes:
            b.w = op
            b.r = []
        return op

    def op(self, eng, fn, reads=(), writes=()):
        return self._add(eng, fn, list(reads), list(writes))

    def dma(self, eng, out, in_, reads=(), writes=(), stream=None):
        return self._add(eng, lambda e: e.dma_start(out=out, in_=in_), list(reads), list(writes),
                         dma=True, stream=stream or ("q_" + eng))

    def barrier(self):
        lasts = [o for k, o in self.last.items() if k not in self.hist]
        for h in self.hist.values():
            lasts.extend(h[-DMA_WIN:])
        for e in ENGS:
            self._mk(e, lambda eng: eng.nop(), lasts, False, None)
        self.epoch += 1

    def emit(self):
        nc = self.nc
        sems = {}
        counters = {}
        for op in self.all:
            if not op.signal:
                continue
            if op.widx is not None:
                key = (op.stream, 1000 + op.widx % DMA_WIN)
            else:
                key = ((op.stream if op.dma else op.eng), op.epoch)
            if key not in sems:
                sems[key] = nc.alloc_semaphore(name="s_%s_%d" % key)
            op.sem = sems[key]
            counters[key] = counters.get(key, 0) + (16 if op.dma else 1)
            op.count = counters[key]
        self.nsems = len(sems)
        self.maxcount = max(counters.values()) if counters else 0
        with nc.Block() as block:
            for ename in ENGS:
                oplist = self.ops[ename]
                if not oplist:
                    continue

                def body(eng, oplist=oplist):
                    waited = {}
                    for op in oplist:
                        need = {}
                        for d in op.deps:
                            k = id(d.sem)
                            if k not in need or need[k][1] < d.count:
                                need[k] = (d.sem, d.count)
                        for k, (sem, cnt) in need.items():
                            if waited.get(k, 0) >= cnt:
                                continue
                            eng.wait_ge(sem, cnt)
                            waited[k] = cnt
                        ins = op.fn(eng)
                        if op.signal:
                            ins.then_inc(op.sem, 16 if op.dma else 1)

                getattr(block, ENGMAP[ename])(body)


class TT:
    def __init__(self, t, b):
        self.t = t
        self.b = b

    def __getitem__(self, idx):
        return self.t[idx]


def build(nlay=DEPTH, dbg=False, stop=None):
    nc = bass.Bass("TRN2", target_bir_lowering=False)
    P = Prog(nc)
    uid = [0]

    def nm(s):
        uid[0] += 1
        return "%s_%d" % (s, uid[0])

    def dram(name, shape, dt, kind="Internal"):
        return nc.dram_tensor(name, shape, dt, kind=kind).ap()

    def sb(name, shape, dt, stack=None):
        if stack is None:
            t = nc.alloc_sbuf_tensor(nm(name), shape, dt)
        else:
            t = stack.enter_context(nc.sbuf_tensor(nm(name), shape, dt))
        return TT(t, P.buf(name))

    ein = lambda n, s: dram(n, s, F32, "ExternalInput")
    x_in = ein("x", [SEQ, D])
    c_in = ein("c", [8, 128])
    ctx_in = ein("ctx", [NCTX, D])
    cctx_in = ein("c_ctx", [8, 128])
    w_ada = ein("w_ada", [DEPTH, D, 9 * D])
    b_ada = ein("b_ada", [DEPTH, 72, 128])
    g_norm = ein("g_norm", [DEPTH, 24, 128])
    w_ffn_in = ein("w_ffn_in", [DEPTH, 2, D, 2 * DFF])
    w_ffn_out = ein("w_ffn_out", [DEPTH, 2, DFF, D])
    w_in = ein("w_in", [DEPTH, D, INW])
    w_out = ein("w_out", [DEPTH, D, D])
    w_dw = ein("w_dw", [DEPTH, 62, 128])
    b_dw = ein("b_dw", [DEPTH, 2, 128])
    cn_g = ein("conv_norm_g", [DEPTH, 2, 128])
    cn_b = ein("conv_norm_b", [DEPTH, 2, 128])
    w_gg = ein("w_gla_gate", [DEPTH, 2, 16, 128])
    b_gg = ein("b_gla_gate", [DEPTH, 2, 128])
    gla_g = ein("gla_norm_g", [DEPTH, 2, 128])
    qn_g = ein("q_norm_g", [DEPTH, 1, 64])
    kn_g = ein("k_norm_g", [DEPTH, 1, 64])
    out = dram("out", [SEQ, D], F32, "ExternalOutput")

    skind = "ExternalOutput" if dbg else "Internal"
    XT = dram("XT", [128, 8, NT], F32, skind)
    PQ = dram("PQ", [128, 4, NT], BF16, skind)
    PK = dram("PK", [128, NT], BF16, skind)
    PVA = dram("PVA", [128, 34, 128], BF16, skind)
    PVB = dram("PVB", [128, 34, 128], BF16, skind)
    PC = dram("PC", [128, 2, PCW], BF16, skind)
    PGF = dram("PGF", [128, 4, NT], BF16, skind)
    PG67 = dram("PG67", [32, NT], BF16, skind)
    PGT = dram("PGT", [128, 34, 384], BF16, skind)
    PGVP = dram("PGVP", [128, 34, 512], BF16, skind)
    OFD = dram("OFD", [2, 128, 2, NT], F32, skind)
    YCG = dram("YCG", [128, 4, NT], BF16, skind)
    COS = dram("COS", [128, NT], F32, skind)
    SIN = dram("SIN", [128, NT], F32, skind)
    W1s = [[dram("W1s_%d_%d" % (l, f), [11, 128, 8, 2, 256], BF16) for f in range(2)] for l in range(nlay)]
    W2s = [[dram("W2s_%d_%d" % (l, f), [8, 128, 22, 128], BF16) for f in range(2)] for l in range(nlay)]
    Wfm = [dram("Wfm_%d" % l, [128, 8, 1696], BF16) for l in range(nlay)]
    Wtm = [dram("Wtm_%d" % l, [128, 8, 512], BF16) for l in range(nlay)]
    Wos = [dram("Wos_%d" % l, [128, 8, 1024], BF16) for l in range(nlay)]
    dB = {}

    def db(name, t=0):
        k = (name, t)
        if k not in dB:
            dB[k] = P.buf("%s_%d" % (name, t))
        return dB[k]

    ps = [TT(nc.alloc_psum_tensor("ps%d" % i, [128, 512], F32), P.buf("ps%d" % i)) for i in range(8)]
    rr = {"wk": 0, "acc": 0, "st": 0, "as": 0}
    BM = {"split": False}

    def bank(kind):
        i = rr[kind]
        rr[kind] += 1
        if BM["split"]:
            if kind == "as":
                return ps[i % 2]
            if kind == "wk":
                return ps[4 + i % 2]
            return ps[6] if kind == "acc" else ps[7]
        if kind in ("wk", "as"):
            return ps[i % 4]
        if kind == "acc":
            return ps[4] if BM.get("mod") else ps[4 + i % 2]
        return ps[6 + i % 2]

    def mm(o, lhsT, rhs, start, stop, R, W):
        P.op("pe", lambda e: e.matmul(o, lhsT=lhsT, rhs=rhs, start=start, stop=stop), R, W)

    def act(o, i, func, R, W, eng="act", **kw):
        P.op(eng, lambda e: e.activation(out=o, in_=i, func=func, **kw), R, W)

    def tt(eng, o, a, b_, op, R, W):
        P.op(eng, lambda e: e.tensor_tensor(out=o, in0=a, in1=b_, op=op), R, W)

    def ts(eng, o, a, s1, s2, op0, op1, R, W):
        if op1 is None:
            P.op(eng, lambda e: e.tensor_scalar(out=o, in0=a, scalar1=s1, scalar2=None, op0=op0), R, W)
        else:
            P.op(eng, lambda e: e.tensor_scalar(out=o, in0=a, scalar1=s1, scalar2=s2, op0=op0, op1=op1), R, W)

    def stt(eng, o, a, s, b_, op0, op1, R, W):
        P.op(eng, lambda e: e.scalar_tensor_tensor(out=o, in0=a, scalar=s, in1=b_, op0=op0, op1=op1), R, W)

    def cp(eng, o, i, R, W):
        if eng == "act":
            P.op(eng, lambda e: e.copy(out=o, in_=i), R, W)
        else:
            P.op(eng, lambda e: e.tensor_copy(out=o, in_=i), R, W)

    def ms(eng, o, v, W):
        P.op(eng, lambda e: e.memset(o, v), [], W)

    def asel(o, i, pattern, cmp, base, cm, R, W, fill=0.0):
        P.op("pool", lambda e: e.affine_select(out=o, in_=i, pattern=pattern, compare_op=cmp, fill=fill,
                                               base=base, channel_multiplier=cm), R, W)

    def rsqrt_act(o, i, R, W):
        act(o, i, AF.Ln, R, W, bias=EPS_AP[:, 0:1])
        act(o, o, AF.Exp, W, W, scale=-0.5)

    ident = sb("ident", [128, 128], F32)
    ms("pool", ident[:], 1.0, [ident.b])
    asel(ident[:], ident[:], [[-1, 128]], ALU.is_equal, 0, 1, [ident.b], [ident.b])
    epsT = sb("eps", [128, 1], F32)
    ms("pool", epsT[:], EPS, [epsT.b])
    EPS_AP = epsT.t
    ones_mean = sb("ones_mean", [128, 128], BF16)
    ms("pool", ones_mean[:], 1.0 / 1024.0, [ones_mean.b])
    blkb = sb("blk64b", [128, 128], BF16)
    blkf = sb("blk64f", [128, 128], F32)
    for t_ in (blkb, blkf):
        ms("pool", t_[:], 1.0 / 64.0, [t_.b])
        ms("pool", t_[0:64, 64:128], 0.0, [t_.b])
        ms("pool", t_[64:128, 0:64], 0.0, [t_.b])
    d1 = sb("d1", [128, 128], F32)
    d2 = sb("d2", [128, 128], F32)
    rrot = sb("rrot", [128, 128], F32)
    ms("pool", d1[:], 1.0, [d1.b])
    asel(d1[:], d1[:], [[-1, 128]], ALU.is_equal, -16, 1, [d1.b], [d1.b])
    ms("pool", d1[:].rearrange("p (b h f) -> p b h f", b=4, h=2, f=16)[:, :, 1, :], 0.0, [d1.b])
    ms("pool", d2[:], 1.0, [d2.b])
    asel(d2[:], d2[:], [[-1, 128]], ALU.is_equal, 16, 1, [d2.b], [d2.b])
    ms("pool", d2[:].rearrange("p (b h f) -> p b h f", b=4, h=2, f=16)[:, :, 0, :], 0.0, [d2.b])
    tt("pool", rrot[:], d2[:], d1[:], ALU.subtract, [d1.b, d2.b], [rrot.b])
    Uc, Mend, mask4 = [], [], []
    for d_ in range(2):
        u = sb("Uc%d" % d_, [128, 128], F32)
        m_ = sb("Mend%d" % d_, [128, 128], F32)
        m4 = sb("mask4_%d" % d_, [128, 4, 128], F32)
        ms("pool", u[:], 1.0, [u.b])
        ms("pool", m_[:], 1.0, [m_.b])
        if d_ == 0:
            asel(u[:], u[:], [[1, 128]], ALU.is_ge, 0, -1, [u.b], [u.b])
            asel(m_[:], m_[:], [[-1, 128]], ALU.is_ge, -1, 1, [m_.b], [m_.b])
        else:
            asel(u[:], u[:], [[-1, 128]], ALU.is_ge, 0, 1, [u.b], [u.b])
            asel(m_[:], m_[:], [[1, 128]], ALU.is_ge, -1, -1, [m_.b], [m_.b])
        for t_ in (u, m_):
            ms("pool", t_[0:64, 64:128], 0.0, [t_.b])
            ms("pool", t_[64:128, 0:64], 0.0, [t_.b])
        for h in range(4):
            cp("pool", m4[:, h, :], u[:], [u.b], [m4.b])
        ts("pool", u[:], u[:], -1.0 / 16.0, None, ALU.mult, None, [u.b], [u.b])
        ts("pool", m_[:], m_[:], -1.0 / 16.0, None, ALU.mult, None, [m_.b], [m_.b])
        Uc.append(u)
        Mend.append(m_)
        mask4.append(m4)
    cind = sb("cind", [128, 2], F32)
    ms("pool", cind[:], -1.0 / 16.0, [cind.b])
    ms("pool", cind[0:64, 1:2], 0.0, [cind.b])
    ms("pool", cind[64:128, 0:1], 0.0, [cind.b])
    hmask = sb("hmask", [128, 4], F32)
    ms("pool", hmask[:], 1.0, [hmask.b])
    asel(hmask[:], hmask[:], [[-32, 4]], ALU.is_ge, 0, 1, [hmask.b], [hmask.b])
    asel(hmask[:], hmask[:], [[32, 4]], ALU.is_ge, 31, -1, [hmask.b], [hmask.b])
    bmask = sb("bmask", [128, 4, 64], F32)
    ms("pool", bmask[:], 1.0, [bmask.b])
    asel(bmask[:], bmask[:], [[-32, 4], [0, 64]], ALU.is_ge, 0, 1, [bmask.b], [bmask.b])
    asel(bmask[:], bmask[:], [[32, 4], [0, 64]], ALU.is_ge, 31, -1, [bmask.b], [bmask.b])

    WBd = {}

    def wbuf(l, name):
        k = (l, name)
        if k not in WBd:
            WBd[k] = P.buf("wc%d_%s" % (l % 3, name))
        return WBd[k]

    def cast_list(l):
        ops = []

        def cast(o, i, wb):
            ops.append(lambda: P.dma("pool", o, i, [], [wb], stream="cast_" + wb.name))

        def ffn_w(f):
            src = w_ffn_in[l, f].rearrange("(k p) n -> p k n", p=128)
            for g in range(11):
                for au in range(2):
                    c0 = au * DFF + g * 256
                    cast(W1s[l][f][g, :, :, au, :], src[:, :, c0:c0 + 256], wbuf(l, "w1_%d" % f))
            src2 = w_ffn_out[l, f].rearrange("(j p) n -> p j n", p=128)
            for m in range(8):
                cast(W2s[l][f][m], src2[:, :, m * 128:(m + 1) * 128], wbuf(l, "w2_%d" % f))

        ffn_w(0)
        src = w_in[l].rearrange("(k p) n -> p k n", p=128)
        wb = wbuf(l, "win")
        cast(Wfm[l][:, :, 0:768], src[:, :, 0:768], wb)
        cast(Wfm[l][:, :, 768:1024], src[:, :, 1024:1280], wb)
        for h in range(8):
            c0 = (8 + h % 4) * 128 + (h // 4) * 64
            cast(Wfm[l][:, :, c0:c0 + 64], src[:, :, 1312 + 64 * h:1312 + 64 * h + 64], wb)
        cast(Wfm[l][:, :, 1536:1664], src[:, :, 1824:1952], wb)
        cast(Wfm[l][:, :, 1664:1696], src[:, :, 1280:1312], wb)
        cast(Wtm[l][:, :, 0:384], src[:, :, 640:1024], wb)
        cast(Wtm[l][:, :, 384:512], src[:, :, 1952:2080], wb)
        wb = wbuf(l, "wout")
        cast(Wos[l][:, 0:4, :], w_out[l, 0:512, :].rearrange("(k p) n -> p k n", p=128), wb)
        for c_ in range(4):
            for hf in range(2):
                r0 = 512 + (c_ + 4 * hf) * 64
                cast(Wos[l][hf * 64:(hf + 1) * 64, 4 + c_, :], w_out[l, r0:r0 + 64, :], wb)
        ffn_w(1)
        return ops

    pending_casts = []

    def emit_casts(n):
        for _ in range(min(n, len(pending_casts))):
            pending_casts.pop(0)()


    PRMA = [sb("PRMA", [128, 108], F32) for _ in range(nlay)]
    PRMB = [sb("PRMB", [128, 62], F32) for _ in range(nlay)]
    MOD = [sb("MOD", [128, 72, 2], F32) for _ in range(nlay)]
    AA = [sb("AA", [128, 3, 8, 2], F32) for _ in range(nlay)]
    GH = [sb("GH", [128, 3, 8, 2], F32) for _ in range(nlay)]
    WG = [sb("wgb", [33, 2, 128], BF16) for _ in range(nlay)]
    S32 = sb("S32", [128, 16], F32)

    def mod_gen(l, ring, bm):
        pa = PRMA[l]
        wi = 0
        for k in range(8):
            for q in range(4):
                slot = ring[wi % len(ring)]
                wi += 1
                P.dma("sp", slot[:], w_ada[l, k * 128:(k + 1) * 128, q * 2304:(q + 1) * 2304], [], [slot.b])
                for n_ in range(18):
                    j = q * 18 + n_
                    mm(bm[:, 2 * j:2 * j + 2], slot[:, n_ * 128:(n_ + 1) * 128], S32[:, k:16:8],
                       k == 0 and j == 0, k == 7 and j == 71, [slot.b, S32.b], [bm.b])
                yield
        mod, aa, gh = MOD[l], AA[l], GH[l]
        for who in range(2):
            tt("dve", mod[:, :, who], bm[:, who:144:2], pa[:, 0:72], ALU.add, [bm.b, pa.b], [mod.b])
        for s in range(3):
            for who in range(2):
                stt("dve", aa[:, s, :, who], mod[:, (3 * s + 1) * 8:(3 * s + 2) * 8, who], 1.0,
                    pa[:, 72 + s * 8:72 + s * 8 + 8], ALU.add, ALU.mult, [mod.b, pa.b], [aa.b])
                ts("dve", gh[:, s, :, who], mod[:, (3 * s + 2) * 8:(3 * s + 3) * 8, who],
                   0.25 if s != 1 else 1.0, None, ALU.mult, None, [mod.b], [gh.b])
    with ExitStack() as st0:
        stc = sb("stc", [16, 128], F32, st0)
        P.dma("sp", stc[0:8, :], c_in, [], [stc.b])
        P.dma("sp", stc[8:16, :], cctx_in, [], [stc.b])
        b_ = bank("st")
        P.op("pe", lambda e: e.transpose(b_[:, 0:16], stc[0:16, :], ident[0:16, 0:16]), [stc.b, ident.b], [b_.b])
        act(S32[:], b_[:, 0:16], AF.Silu, [b_.b], [S32.b])
        wada_ring = [sb("wada%d" % i, [128, 2304], F32, st0) for i in range(3)]
        for l in range(nlay):
            sa = sb("stA", [128, 128], F32, st0)
            sbt = sb("stB", [64, 128], F32, st0)
            ld = lambda o, i, t_: P.dma("sp", o, i, [], [t_.b])
            ld(sa[0:72, :], b_ada[l], sa)
            ld(sa[72:96, :], g_norm[l], sa)
            ld(sa[96:98, :], b_dw[l], sa)
            ld(sa[98:100, :], cn_g[l], sa)
            ld(sa[100:102, :], cn_b[l], sa)
            ld(sa[102:104, :], b_gg[l], sa)
            ld(sa[104:106, :], gla_g[l], sa)
            ld(sa[106:107, 0:64], qn_g[l], sa)
            ld(sa[106:107, 64:128], qn_g[l], sa)
            ld(sa[107:108, 0:64], kn_g[l], sa)
            ld(sa[107:108, 64:128], kn_g[l], sa)
            ld(sbt[0:62, :], w_dw[l], sbt)
            pa = PRMA[l]
            pb = PRMB[l]
            b1 = bank("st")
            P.op("pe", lambda e, b1=b1, sa=sa: e.transpose(b1[:, 0:108], sa[0:108, :], ident[0:108, 0:108]),
                 [sa.b, ident.b], [b1.b])
            cp("dve", pa[:], b1[:, 0:108], [b1.b], [pa.b])
            b2 = bank("st")
            P.op("pe", lambda e, b2=b2, sbt=sbt: e.transpose(b2[:, 0:62], sbt[0:62, :], ident[0:62, 0:62]),
                 [sbt.b, ident.b], [b2.b])
            cp("dve", pb[:], b2[:, 0:62], [b2.b], [pb.b])
            wg32 = sb("wg32", [33, 2, 128], F32, st0)
            wgb = WG[l]
            ms("dve", wg32[:], 0.0, [wg32.b])
            P.dma("sp", wg32[0:16, 0, :], w_gg[l, 0], [], [wg32.b])
            P.dma("sp", wg32[16:32, 1, :], w_gg[l, 1], [], [wg32.b])
            P.dma("sp", wg32[32:33, :, :], b_gg[l:l + 1], [], [wg32.b])
            cp("dve", wgb[:], wg32[:], [wg32.b], [wgb.b])
            if l == 0:
                for _ in mod_gen(0, wada_ring, bank("acc")):
                    pass

        I32 = mybir.dt.int32
        pidx = sb("pidx", [128, 1], F32, st0)
        P.op("pool", lambda e: e.iota(pidx[:], pattern=[[0, 1]], base=0, channel_multiplier=1,
                                      allow_small_or_imprecise_dtypes=True), [], [pidx.b])
        freq = sb("freq", [128, 1], F32, st0)
        axm = sb("axm", [128, 1], F32, st0)
        pti = sb("pti", [128, 1], I32, st0)
        ptf = sb("ptf", [128, 1], F32, st0)

        def floordiv(dst, src, div, ti):
            ts("dve", dst[:], src[:], -(div - 1) / 2.0, 1.0 / div, ALU.add, ALU.mult, [src.b], [dst.b])
            cp("dve", ti[:], dst[:], [dst.b], [ti.b])
            cp("dve", dst[:], ti[:], [ti.b], [dst.b])

        floordiv(ptf, pidx, 16.0, pti)
        stt("dve", freq[:], ptf[:], -16.0, pidx[:], ALU.mult, ALU.add, [ptf.b, pidx.b], [freq.b])
        act(freq[:], freq[:], AF.Exp, [freq.b], [freq.b], scale=-math.log(10000.0) / 16.0)
        floordiv(ptf, pidx, 64.0, pti)
        stt("dve", axm[:], ptf[:], -64.0, pidx[:], ALU.mult, ALU.add, [ptf.b, pidx.b], [axm.b])
        ts("dve", axm[:], axm[:], 32.0, None, ALU.is_ge, None, [axm.b], [axm.b])
        tok = sb("tok", [128, SEQ], F32, st0)
        colp = sb("colp", [128, SEQ], F32, st0)
        rowp = sb("rowp", [128, SEQ], F32, st0)
        toki = sb("toki", [128, SEQ], I32, st0)
        P.op("pool", lambda e: e.iota(tok[:], pattern=[[1, SEQ]], base=0, channel_multiplier=0,
                                      allow_small_or_imprecise_dtypes=True), [], [tok.b])
        pending_casts.extend(cast_list(0))
        emit_casts(1000)
        floordiv(rowp, tok, 64.0, toki)
        stt("dve", colp[:], rowp[:], -64.0, tok[:], ALU.mult, ALU.add, [rowp.b, tok.b], [colp.b])
        tt("dve", colp[:], colp[:], rowp[:], ALU.subtract, [colp.b, rowp.b], [colp.b])
        stt("dve", tok[:], colp[:], axm[:, 0:1], rowp[:], ALU.mult, ALU.add, [colp.b, rowp.b, axm.b], [tok.b])
        ts("dve", tok[:], tok[:], freq[:, 0:1], None, ALU.mult, None, [tok.b, freq.b], [tok.b])
        PI = math.pi

        def sin_of(dst, shift, dram_dst):
            ts("dve", rowp[:], tok[:], shift, None, ALU.add, None, [tok.b], [rowp.b])
            ts("dve", dst[:], rowp[:], 1.0 / (2 * PI), None, ALU.mult, None, [rowp.b], [dst.b])
            cp("dve", toki[:], dst[:], [dst.b], [toki.b])
            cp("dve", dst[:], toki[:], [toki.b], [dst.b])
            stt("dve", dst[:], dst[:], -2 * PI, rowp[:], ALU.mult, ALU.add, [dst.b, rowp.b], [dst.b])
            ts("dve", dst[:], dst[:], PI, -PI, ALU.min, ALU.max, [dst.b], [dst.b])
            act(dst[:], dst[:], AF.Sin, [dst.b], [dst.b])
            P.dma("sp", dram_dst, dst[:], [dst.b], [db("COS")])

        sin_of(colp, 0.0, SIN[:, NCTX:NT])
        sin_of(colp, 0.5 * PI, COS[:, NCTX:NT])
        one_t = sb("one_t", [128, NCTX], F32, st0)
        zero_t = sb("zero_t", [128, 2, 32], BF16, st0)
        ms("dve", one_t[:], 1.0, [one_t.b])
        P.dma("sp", COS[:, 0:NCTX], one_t[:], [one_t.b], [db("COS")])
        zt2 = sb("zt2", [128, NCTX], F32, st0)
        ms("dve", zt2[:], 0.0, [zt2.b])
        P.dma("sp", SIN[:, 0:NCTX], zt2[:], [zt2.b], [db("COS")])
        ms("dve", zero_t[:], 0.0, [zero_t.b])
        for c0 in (0, 15 + 256, 15 + 256 + 30 + 4096):
            wdt = 15 if c0 != 15 + 256 else 30
            P.dma("sp", PC[:, :, c0:c0 + wdt], zero_t[:, :, 0:wdt], [zero_t.b], [db("PChalo")])
        P.barrier()

    if dbg:
        for nm_, t_, shp in (("D_PRMA", PRMA[0], [128, 108]), ("D_MOD", MOD[0], [128, 144]), ("D_AA", AA[0], [128, 48]),
                             ("D_GH", GH[0], [128, 48])):
            dd_ = dram(nm_, shp, F32, "ExternalOutput")
            src_ = t_[:] if len(t_.t.shape) == 2 else t_[:].rearrange("p a b -> p (a b)") if len(t_.t.shape) == 3 \
                else t_[:].rearrange("p a b c -> p (a b c)")
            P.dma("sp", dd_, src_, [t_.b], [db(nm_)])
    if stop == "setup":
        return finish(nc, P)

    def prenorm(S, X, w, l, s, who):
        act(S.SQ[:, :, :w], X[:, :, :w], AF.Square, [X.b], [S.SQ.b])
        b_ = bank("st")
        for k in range(8):
            mm(b_[:, :w], ones_mean[:], S.SQ[:, k, :w], k == 0, k == 7, [ones_mean.b, S.SQ.b], [b_.b])
        rsqrt_act(S.RSTD[:, :w], b_[:, :w], [b_.b, epsT.b], [S.RSTD.b])
        for k in range(8):
            tmp = S.tmp()
            stt("dve", tmp[:, :w], X[:, k, :w], AA[l][:, s, k, who:who + 1], S.RSTD[:, :w], ALU.mult, ALU.mult,
                [X.b, AA[l].b, S.RSTD.b], [tmp.b])
            ts("dve", S.Hn[:, k, :w], tmp[:, :w], MOD[l][:, 3 * s * 8 + k, who:who + 1], None, ALU.add, None,
               [tmp.b, MOD[l].b], [S.Hn.b])

    def ffn(S, X, w, l, f, s, who):
        prenorm(S, X, w, l, s, who)
        yield
        for g in range(11):
            slot = S.wslot()
            P.dma("sp", slot[:], W1s[l][f][g].rearrange("p k a c -> p (k a c)"), [wbuf(l, "w1_%d" % f)], [slot.b], stream="wld")
            sv = slot[:].rearrange("p (k a c) -> p k a c", k=8, a=2, c=256)
            for jj in range(2):
                j = 2 * g + jj
                pa_, pu_ = bank("wk"), bank("wk")
                for k in range(8):
                    mm(pa_[:, :w], sv[:, k, 0, jj * 128:(jj + 1) * 128], S.Hn[:, k, :w], k == 0, k == 7,
                       [slot.b, S.Hn.b], [pa_.b])
                for k in range(8):
                    mm(pu_[:, :w], sv[:, k, 1, jj * 128:(jj + 1) * 128], S.Hn[:, k, :w], k == 0, k == 7,
                       [slot.b, S.Hn.b], [pu_.b])
                tmp = S.tmp()
                act(tmp[:, :w], pa_[:, :w], AF.Tanh, [pa_.b], [tmp.b], scale=0.5)
                tmp2 = S.tmp()
                stt("dve", tmp2[:, :w], tmp[:, :w], 1.0, pa_[:, :w], ALU.add, ALU.mult, [tmp.b, pa_.b], [tmp2.b])
                tt("dve", S.ACT[:, j, :w], tmp2[:, :w], pu_[:, :w], ALU.mult, [tmp2.b, pu_.b], [S.ACT.b])
                yield
        for m in range(8):
            slot = S.w2slot()
            P.dma("sp", slot[:], W2s[l][f][m], [wbuf(l, "w2_%d" % f)], [slot.b], stream="wld")
            po = bank("acc")
            for j in range(NJ):
                mm(po[:, :w], slot[:, j, :], S.ACT[:, j, :w], j == 0, j == NJ - 1, [slot.b, S.ACT.b], [po.b])
            stt("dve", X[:, m, :w], po[:, :w], GH[l][:, s, m, who:who + 1], X[:, m, :w], ALU.mult, ALU.add,
                [po.b, GH[l].b, X.b], [X.b])
            yield

    def headnorm_rope(S, pb_, w, gcol, l, dst, dstb, n0):
        sq = S.tmpb()
        act(sq[:, :w], pb_[:, :w], AF.Square, [pb_.b], [sq.b])
        b2 = bank("st")
        mm(b2[:, :w], blkb[:], sq[:, :w], True, True, [blkb.b, sq.b], [b2.b])
        rs = S.tmp()
        rsqrt_act(rs[:, :w], b2[:, :w], [b2.b, epsT.b], [rs.b])
        qn = S.tmp()
        stt("dve", qn[:, :w], pb_[:, :w], PRMA[l][:, gcol:gcol + 1], rs[:, :w], ALU.mult, ALU.mult,
            [pb_.b, PRMA[l].b, rs.b], [qn.b])
        b3 = bank("st")
        mm(b3[:, :w], rrot[:], qn[:, :w], True, True, [rrot.b, qn.b], [b3.b])
        t1 = S.tmp()
        tt("pool", t1[:, :w], qn[:, :w], S.COS[:, :w], ALU.mult, [qn.b, S.COS.b], [t1.b])
        t2 = S.tmp()
        tt("dve", t2[:, :w], b3[:, :w], S.SIN[:, :w], ALU.mult, [b3.b, S.SIN.b], [t2.b])
        tt("pool", dst, t1[:, :w], t2[:, :w], ALU.add, [t1.b, t2.b], [dstb])

    def project(S, X, t, n0, w, l, who):
        prenorm(S, X, w, l, 1, who)
        yield
        nsub = w // 128
        kc0 = n0 // 128
        Hn = S.Hn

        def piece(c0, ncol, src):
            slot = S.wslot()
            P.dma("sp", slot[:, 0:8 * ncol].rearrange("p (k c) -> p k c", k=8), src[:, :, c0:c0 + ncol],
                  [wbuf(l, "win")], [slot.b], stream="wld")
            return slot, slot[:, 0:8 * ncol].rearrange("p (k c) -> p k c", k=8)

        def proj(sv, slot, c, m=128):
            pb_ = bank("wk")
            for k in range(8):
                mm(pb_[0:m, :w], sv[:, k, c * 128:c * 128 + m], Hn[:, k, :w], k == 0, k == 7, [slot.b, Hn.b], [pb_.b])
            return pb_

        def gated(pg_, pa_, dst, dstb):
            tmp = S.tmp()
            act(tmp[:, :w], pg_[:, :w], AF.Tanh, [pg_.b], [tmp.b], scale=0.5)
            tmp2 = S.tmp()
            src = pa_ if pa_ is not None else pg_
            stt("dve", tmp2[:, :w], tmp[:, :w], 1.0, src[:, :w], ALU.add, ALU.mult, [tmp.b, src.b], [tmp2.b])
            ts("dve", dst, tmp2[:, :w], 0.5, None, ALU.mult, None, [tmp2.b], [dstb])

        P.dma("sp", S.COS[:, :w], COS[:, n0:n0 + w], [db("COS")], [S.COS.b])
        P.dma("sp", S.SIN[:, :w], SIN[:, n0:n0 + w], [db("COS")], [S.SIN.b])
        slot, sv = piece(1024, 512, Wfm[l])
        for c in range(4):
            pb_ = proj(sv, slot, c)
            headnorm_rope(S, pb_, w, 106, l, S.stQ[:, c, :w], S.stQ.b, n0)
            yield
        P.dma("act", PQ[:, :, n0:n0 + w], S.stQ[:, :, :w], [S.stQ.b], [db("PQ", t)], stream="st")
        slot, sv = piece(1536, 160, Wfm[l])
        pb_ = proj(sv, slot, 0)
        headnorm_rope(S, pb_, w, 107, l, S.stK[:, :w], S.stK.b, n0)
        P.dma("act", PK[:, n0:n0 + w], S.stK[:, :w], [S.stK.b], [db("PK", t)], stream="st")
        pb_ = proj(sv, slot, 1, m=32)
        cp("dve", S.st67[0:32, :w], pb_[0:32, :w], [pb_.b], [S.st67.b])
        P.dma("act", PG67[:, n0:n0 + w], S.st67[0:32, :w], [S.st67.b], [db("PG67", t)], stream="st")
        yield
        slot, sv = piece(0, 512, Wfm[l])
        for cc in range(2):
            pa_ = proj(sv, slot, cc)
            pg_ = proj(sv, slot, 2 + cc)
            gated(pg_, pa_, S.stC[:, cc, :w], S.stC.b)
            yield
        P.dma("act", PC[:, :, pc_off(n0):pc_off(n0) + w], S.stC[:, :, :w], [S.stC.b], [db("PC", t)], stream="st")
        slot, sv = piece(512, 512, Wfm[l])
        pb_ = proj(sv, slot, 0)
        act(S.stG[:, 0, :w], pb_[:, :w], AF.Identity, [pb_.b], [S.stG.b], scale=32.0 ** -0.5)
        pb_ = proj(sv, slot, 1)
        cp("dve", S.stG[:, 1, :w], pb_[:, :w], [pb_.b], [S.stG.b])
        yield
        for cc in range(2):
            pb_ = proj(sv, slot, 2 + cc)
            gated(pb_, None, S.stG[:, 2 + cc, :w], S.stG.b)
            yield
        P.dma("act", PGF[:, :, n0:n0 + w], S.stG[:, :, :w], [S.stG.b], [db("PGF", t)], stream="st")
        slot, sv = piece(0, 512, Wtm[l])
        for j in range(nsub):
            pb_ = bank("wk")
            for k in range(8):
                mm(pb_[:, :], Hn[:, k, j * 128:(j + 1) * 128], sv[:, k, :], k == 0, k == 7, [slot.b, Hn.b], [pb_.b])
            cp("dve", S.stT[:, j, :], pb_[:, 0:384], [pb_.b], [S.stT.b])
            vp = S.stVP[:, j, :].rearrange("p (h two d) -> p h two d", h=4, two=2, d=64)
            src = pb_[:, 128:384].rearrange("p (h d) -> p h d", h=4, d=64)
            cp("dve", vp[:, 0:4:2, 0, :], src[:, 0:4:2, :], [pb_.b], [S.stVP.b])
            cp("act", vp[:, 1:4:2, 1, :], src[:, 1:4:2, :], [pb_.b], [S.stVP.b])
            cp("act", S.stVA[:, j, 0:64], pb_[:, 384:448], [pb_.b], [S.stVA.b])
            cp("dve", S.stVB[:, j, 64:128], pb_[:, 448:512], [pb_.b], [S.stVB.b])
            yield
        P.dma("act", PGT[:, kc0:kc0 + nsub, :], S.stT[:, 0:nsub, :], [S.stT.b], [db("PGT", t)], stream="st")
        P.dma("act", PGVP[:, kc0:kc0 + nsub, :], S.stVP[:, 0:nsub, :], [S.stVP.b], [db("PGVP", t)], stream="st")
        P.dma("act", PVA[:, kc0:kc0 + nsub, :], S.stVA[:, 0:nsub, :], [S.stVA.b], [db("PVA", t)], stream="st")
        P.dma("act", PVB[:, kc0:kc0 + nsub, :], S.stVB[:, 0:nsub, :], [S.stVB.b], [db("PVB", t)], stream="st")

    def attention(S, t, n0, w):
        P.dma("sp", S.Q[:, :, :w], PQ[:, :, n0:n0 + w], [db("PQ", t)], [S.Q.b])
        kcs = list(range(2)) if t == 0 else list(range(34))
        OA, OB = (ps[2], ps[3]) if BM["split"] else (ps[4], ps[5])
        for c in range(4):
            pend = None
            for idx, kc in enumerate(kcs):
                sA, sB = bank("as"), bank("as")
                mm(sA[:, :w], S.KT[0:64, kc * 128:(kc + 1) * 128], S.Q[0:64, c, :w], True, True, [S.KT.b, S.Q.b], [sA.b])
                mm(sB[:, :w], S.KT[64:128, kc * 128:(kc + 1) * 128], S.Q[64:128, c, :w], True, True,
                   [S.KT.b, S.Q.b], [sB.b])
                pA, pB = S.pt(), S.pt()
                act(pA[:, :w], sA[:, :w], AF.Exp, [sA.b], [pA.b], scale=0.125)
                act(pB[:, :w], sB[:, :w], AF.Exp, [sB.b], [pB.b], scale=0.125)
                if pend is not None:
                    pend()

                def pv(kc=kc, pA=pA, pB=pB, first=(idx == 0), last=(idx == len(kcs) - 1)):
                    mm(OA[:, :w], S.VA[:, kc, :], pA[:, :w], first, last, [S.VA.b, pA.b], [OA.b])
                    mm(OB[:, :w], S.VB[:, kc, :], pB[:, :w], first, last, [S.VB.b, pB.b], [OB.b])
                pend = pv
                yield
            pend()
            rc = S.tmp()
            P.op("dve", lambda e, rc=rc: e.reciprocal(out=rc[0:64, :w], in_=OA[64:128, :w]), [OA.b], [rc.b])
            P.op("dve", lambda e, rc=rc: e.reciprocal(out=rc[64:128, :w], in_=OB[0:64, :w]), [OB.b], [rc.b])
            tt("dve", S.YA[0:64, c, :w], OA[0:64, :w], rc[0:64, :w], ALU.mult, [OA.b, rc.b], [S.YA.b])
            tt("dve", S.YA[64:128, c, :w], OB[64:128, :w], rc[64:128, :w], ALU.mult, [OB.b, rc.b], [S.YA.b])
            yield

    def wout(S, X, t, n0, w, l, who):
        s1, s2 = S.wslot(), S.wslot()
        P.dma("sp", s1[:].rearrange("p (k c) -> p k c", k=4), Wos[l][:, 0:4, :], [wbuf(l, "wout")], [s1.b], stream="wld")
        P.dma("sp", s2[:].rearrange("p (k c) -> p k c", k=4), Wos[l][:, 4:8, :], [wbuf(l, "wout")], [s2.b], stream="wld")
        v1 = s1[:].rearrange("p (k c) -> p k c", k=4)
        v2 = s2[:].rearrange("p (k c) -> p k c", k=4)
        for m in range(8):
            po = bank("acc")
            for k in range(8):
                sv, sl = (v1, s1) if k < 4 else (v2, s2)
                rhs, rb = (S.YT[:, k, :w], S.YT.b) if k < 4 else (S.YA[:, k - 4, :w], S.YA.b)
                mm(po[:, :w], sv[:, k % 4, m * 128:(m + 1) * 128], rhs, k == 0, k == 7, [sl.b, rb], [po.b])
            stt("dve", X[:, m, :w], po[:, :w], GH[l][:, 1, m, who:who + 1], X[:, m, :w], ALU.mult, ALU.add,
                [po.b, GH[l].b, X.b], [X.b])
            yield

    class SweepBufs:
        pass

    def drain(g):
        for _ in g:
            pass

    def interleave(main, side, ratio):
        acc_ = 0.0
        side_done = side is None
        for _ in main:
            acc_ += ratio
            while not side_done and acc_ >= 1.0:
                acc_ -= 1.0
                try:
                    next(side)
                except StopIteration:
                    side_done = True
        if not side_done:
            drain(side)

    def sweep(l_mix, l_next):
        BM["split"] = l_mix is not None
        with ExitStack() as es:
            S = SweepBufs()
            Xb = [sb("X", [128, 8, 512], F32, es) for _ in range(2)]
            S.Hn = sb("Hn", [128, 8, 512], BF16, es)
            S.ACT = sb("ACT", [128, NJ, 512], BF16, es)
            S.SQ = TT(S.ACT.t[:, 0:8, :], S.ACT.b)
            S.RSTD = sb("RSTD", [128, 512], F32, es)
            tmps = [sb("tmp", [128, 512], F32, es) for _ in range(5)]
            tmpbs = [sb("tmpb", [128, 512], BF16, es) for _ in range(2)]
            wring = [sb("wr", [128, 4096], BF16, es) for _ in range(3)]
            w2ring = [sb("w2r", [128, NJ, 128], BF16, es) for _ in range(2)]
            cnt = {"t": 0, "tb": 0, "w": 0, "w2": 0, "pt": 0}

            def rot(lst, key):
                def f():
                    cnt[key] += 1
                    return lst[cnt[key] % len(lst)]
                return f
            S.tmp, S.tmpb, S.wslot, S.w2slot = rot(tmps, "t"), rot(tmpbs, "tb"), rot(wring, "w"), rot(w2ring, "w2")
            if l_mix is not None:
                S.YT = sb("YT", [128, 4, 512], BF16, es)
                S.YA = sb("YA", [128, 4, 512], BF16, es)
                S.Q = sb("Q", [128, 4, 512], BF16, es)
                S.KT = sb("KT", [128, NT], BF16, es)
                S.VA = sb("VA", [128, 34, 128], BF16, es)
                S.VB = sb("VB", [128, 34, 128], BF16, es)
                pts = [sb("pt", [128, 512], BF16, es) for _ in range(4)]
                S.pt = rot(pts, "pt")
                P.dma("sp", S.KT[:], PK, [db("PK", t) for t in range(9)], [S.KT.b])
                P.dma("sp", S.VA[:], PVA, [db("PVA", t) for t in range(9)], [S.VA.b])
                P.dma("sp", S.VB[:], PVB, [db("PVB", t) for t in range(9)], [S.VB.b])
            if l_next is not None:
                S.COS = sb("COS", [128, 512], F32, es)
                S.SIN = sb("SIN", [128, 512], F32, es)
                S.stC = sb("stC", [128, 2, 512], BF16, es)
                S.stG = sb("stG", [128, 4, 512], BF16, es)
                S.stQ = sb("stQ", [128, 4, 512], BF16, es)
                S.stK = sb("stK", [128, 512], BF16, es)
                S.st67 = sb("st67", [32, 512], BF16, es)
                S.stT = sb("stT", [128, 4, 384], BF16, es)
                S.stVP = sb("stVP", [128, 4, 512], BF16, es)
                S.stVA = sb("stVA", [128, 4, 128], BF16, es)
                S.stVB = sb("stVB", [128, 4, 128], BF16, es)
                ms("pool", S.stVP[:], 0.0, [S.stVP.b])
                ms("pool", S.stVA[:], 1.0, [S.stVA.b])
                ms("pool", S.stVB[:], 1.0, [S.stVB.b])
            if l_mix is None or l_next is None:
                S.xio = [sb("xio", [128, 1024], F32, es) for _ in range(2)]
            if l_next is not None and l_next + 1 < nlay and l_mix is not None:
                pending_casts.extend(cast_list(l_next + 1))
                emit_casts(1000)
            if l_mix is None and nlay > 1:
                pending_casts.extend(cast_list(1))
            nxc = {"n": 0}
            tl = [t for t in range(9) if not (l_next is None and t == 0)]

            def load_x(i):
                t = tl[i]
                n0, w = TILES[t]
                X = Xb[i % 2]
                P.dma("sp", X[:, :, :w], XT[:, :, n0:n0 + w], [db("XT", t)], [X.b])

            def load_y(i):
                t = tl[i]
                n0, w = TILES[t]
                P.dma("sp", S.YT[:, :, :w], YCG[:, :, n0:n0 + w], [db("YCG", t)], [S.YT.b])

            def main_gen(i):
                t = tl[i]
                n0, w = TILES[t]
                who = 1 if t == 0 else 0
                nsub = w // 128
                X = Xb[i % 2]
                emit_casts(11)
                if l_mix is None:
                    for j in range(nsub):
                        xi = S.xio[nxc["n"] % 2]
                        nxc["n"] += 1
                        src = ctx_in[j * 128:(j + 1) * 128, :] if t == 0 else \
                            x_in[n0 - NCTX + j * 128:n0 - NCTX + (j + 1) * 128, :]
                        P.dma("sp", xi[:], src, [], [xi.b])
                        for kh in range(2):
                            b_ = bank("wk")
                            for kk in range(4):
                                k = kh * 4 + kk
                                P.op("pe", lambda e, b_=b_, xi=xi, k=k, kk=kk: e.transpose(
                                    b_[:, kk * 128:(kk + 1) * 128], xi[:, k * 128:(k + 1) * 128], ident[:]),
                                    [xi.b, ident.b], [b_.b])
                            cp("dve" if kh == 0 else "act", X[:, kh * 4:(kh + 1) * 4, j * 128:(j + 1) * 128],
                               b_[:, :].rearrange("p (k n) -> p k n", k=4), [b_.b], [X.b])
                else:
                    yield from wout(S, X, t, n0, w, l_mix, who)
                    if i + 1 < len(tl):
                        load_x(i + 1)
                        load_y(i + 1)
                    yield from ffn(S, X, w, l_mix, 1, 2, who)
                if l_next is not None:
                    yield from ffn(S, X, w, l_next, 0, 0, who)
                    yield from project(S, X, t, n0, w, l_next, who)
                    P.dma("act", XT[:, :, n0:n0 + w], X[:, :, :w], [X.b], [db("XT", t)], stream="st")
                else:
                    for j in range(nsub):
                        xo = S.xio[nxc["n"] % 2]
                        nxc["n"] += 1
                        for kh in range(2):
                            b_ = bank("wk")
                            for kk in range(4):
                                k = kh * 4 + kk
                                P.op("pe", lambda e, b_=b_, k=k, kk=kk, j=j, X=X: e.transpose(
                                    b_[:, kk * 128:(kk + 1) * 128], X[:, k, j * 128:(j + 1) * 128], ident[:]),
                                    [X.b, ident.b], [b_.b])
                            cp("dve" if kh == 0 else "act", xo[:, kh * 512:(kh + 1) * 512], b_[:, :], [b_.b], [xo.b])
                        r0 = n0 - NCTX + j * 128
                        P.dma("act", out[r0:r0 + 128, :], xo[:], [xo.b], [db("out")], stream="st")
                        yield

            if l_mix is not None:
                load_x(0)
                load_y(0)
                drain(attention(S, tl[0], *TILES[tl[0]]))
            for i in range(len(tl)):
                if i == len(tl) - 1:
                    emit_casts(1000)
                side = None
                if l_mix is not None and i + 1 < len(tl):
                    side = attention(S, tl[i + 1], *TILES[tl[i + 1]])
                interleave(main_gen(i), side, 142.0 / (83 if l_next is not None else 42))
            P.barrier()
        BM["split"] = False

    def phase2a(l):
        with ExitStack() as es:
            pa, pb = PRMA[l], PRMB[l]
            Dg = sb("Dg", [128, 62, 128], BF16, es)
            for i in range(62):
                ts("pool" if i % 2 else "dve", Dg[:, i, :], ident[:], pb[:, i:i + 1], None, ALU.mult, None,
                   [ident.b, pb.b], [Dg.b])
            f32t = [sb("c32", [128, 512], F32, es) for _ in range(6)]
            cc_ = {"i": 0}

            def ft():
                cc_["i"] += 1
                return f32t[cc_["i"] % len(f32t)]
            hp = [sb("hpad", [128, 2, 542], BF16, es) for _ in range(2)]
            stY = [sb("stY", [128, 2, 512], BF16, es) for _ in range(2)]

            def conv_tile(t):
                yield
                n0, w = TILES[t]
                h = hp[t % 2]
                rd = [db("PC", tt_) for tt_ in range(max(0, t - 1), min(8, t + 1) + 1)] + [db("PChalo")]
                P.dma("sp", h[:, :, 0:w + 30], PC[:, :, pc_off(n0) - 15:pc_off(n0) + w + 15], rd, [h.b])
                sy = stY[t % 2]
                for cc in range(2):
                    b_ = ps[0]
                    for k in range(31):
                        mm(b_[:, :w], Dg[:, 2 * k + cc, :], h[:, cc, k:k + w], k == 0, k == 30, [Dg.b, h.b], [b_.b])
                    yield
                    cv = ft()
                    act(cv[:, :w], b_[:, :w], AF.Identity, [b_.b, pa.b], [cv.b], bias=pa[:, 96 + cc:97 + cc])
                    b1 = ps[0]
                    mm(b1[:, :w], blkf[:], cv[:, :w], True, True, [blkf.b, cv.b], [b1.b])
                    dd = ft()
                    tt("dve", dd[:, :w], cv[:, :w], b1[:, :w], ALU.subtract, [cv.b, b1.b], [dd.b])
                    dq = ft()
                    tt("pool", dq[:, :w], dd[:, :w], dd[:, :w], ALU.mult, [dd.b], [dq.b])
                    b2 = ps[0]
                    mm(b2[:, :w], blkf[:], dq[:, :w], True, True, [blkf.b, dq.b], [b2.b])
                    rs = ft()
                    rsqrt_act(rs[:, :w], b2[:, :w], [b2.b, epsT.b], [rs.b])
                    tt("dve", dd[:, :w], dd[:, :w], rs[:, :w], ALU.mult, [dd.b, rs.b], [dd.b])
                    ts("pool", dd[:, :w], dd[:, :w], pa[:, 98 + cc:99 + cc], pa[:, 100 + cc:101 + cc], ALU.mult, ALU.add,
                       [dd.b, pa.b], [dd.b])
                    yield
                    th = ft()
                    act(th[:, :w], dd[:, :w], AF.Tanh, [dd.b], [th.b], scale=0.5)
                    stt("dve", th[:, :w], th[:, :w], 1.0, dd[:, :w], ALU.add, ALU.mult, [th.b, dd.b], [th.b])
                    ts("dve", sy[:, cc, :w], th[:, :w], 0.5, None, ALU.mult, None, [th.b], [sy.b])
                P.dma("act", YCG[:, 0:2, n0:n0 + w], sy[:, :, :w], [sy.b], [db("YCG", t)], stream="st")

            class G:
                pass
            gd = []
            for d_ in range(2):
                g = G()
                g.q = [sb("gq", [128, 512], BF16, es) for _ in range(2)]
                g.k = [sb("gk", [128, 512], BF16, es) for _ in range(2)]
                g.p67 = [sb("g67", [33, 512], BF16, es) for _ in range(2)]
                g.tv = [sb("gtv", [128, 4, 384], BF16, es) for _ in range(2)]
                g.vp = [sb("gvp", [128, 4, 512], BF16, es) for _ in range(2)]
                g.so = [sb("gso", [128, 2, 512], F32, es) for _ in range(2)]
                for t_ in g.p67:
                    ms("pool", t_[32:33, :], 1.0, [t_.b])
                g.sc = []
                for par in range(2):
                    s_ = G()
                    s_.E1 = sb("E1", [128, 128], F32, es)
                    s_.SP = sb("SP", [128, 128], F32, es)
                    s_.EQ = sb("EQ", [128, 128], F32, es)
                    s_.EK = sb("EK", [128, 128], F32, es)
                    s_.QE = sb("QE", [128, 128], BF16, es)
                    s_.KE = sb("KE", [128, 4, 128], BF16, es)
                    s_.EE = sb("EE", [128, 128], F32, es)
                    s_.KEND = sb("KEND", [128, 128], BF16, es)
                    s_.DEC = sb("DEC", [128, 2], F32, es)
                    s_.AM = sb("AM", [128, 4, 128], BF16, es)
                    g.sc.append(s_)
                g.S32 = sb("S32", [128, 256], F32, es)
                g.Sb = sb("Sb", [128, 256], BF16, es)
                g.TMP = sb("STMP", [128, 256], F32, es)
                ms("pool", g.S32[:], 0.0, [g.S32.b])
                ms("pool", g.Sb[:], 0.0, [g.Sb.b])
                g.n = 0
                bkB_ = ps[3] if d_ == 0 else ps[6]
                g.bB = (TT(bkB_.t, P.buf("bBlo")), TT(bkB_.t, P.buf("bBhi")))
                gd.append(g)

            def gla_tile(d_, t, it):
                g = gd[d_]
                n0, w = TILES[t]
                nsub = w // 128
                kc0 = n0 // 128
                bi = it % 2
                q, kk_, p67, tv, vp, so = g.q[bi], g.k[bi], g.p67[bi], g.tv[bi], g.vp[bi], g.so[bi]
                P.dma("sp", q[:, :w], PGF[:, 0, n0:n0 + w], [db("PGF", t)], [q.b])
                P.dma("sp", kk_[:, :w], PGF[:, 1, n0:n0 + w], [db("PGF", t)], [kk_.b])
                P.dma("sp", p67[0:32, :w], PG67[:, n0:n0 + w], [db("PG67", t)], [p67.b])
                P.dma("sp", tv[:, 0:nsub, :], PGT[:, kc0:kc0 + nsub, :], [db("PGT", t)], [tv.b])
                P.dma("sp", vp[:, 0:nsub, :], PGVP[:, kc0:kc0 + nsub, :], [db("PGVP", t)], [vp.b])
                subs = list(range(nsub)) if d_ == 0 else list(range(nsub - 1, -1, -1))
                bkA, bkC = (ps[2], ps[4]) if d_ == 0 else (ps[5], ps[7])
                bBlo, bBhi = g.bB
                chunks = (0, 1) if d_ == 0 else (1, 0)

                def SI(j):
                    s_ = g.sc[g.n % 2]
                    g.n += 1
                    cs = slice(j * 128, (j + 1) * 128)
                    mm(bkA[:, 0:128], p67[0:33, cs], WG[l][0:33, d_, :], True, True, [p67.b, WG[l].b], [bkA.b])
                    act(s_.E1[:], bkA[:, 0:128], AF.Exp, [bkA.b], [s_.E1.b], scale=-1.0)
                    act(s_.SP[:], s_.E1[:], AF.Ln, [s_.E1.b], [s_.SP.b], bias=1.0)
                    mm(bBlo[:, 0:128], s_.SP[:], Uc[d_][:], True, True, [s_.SP.b, Uc[d_].b], [bBlo.b])
                    mm(bBlo[:, 128:256], Mend[d_][:], s_.SP[:], True, True, [s_.SP.b, Mend[d_].b], [bBlo.b])
                    mm(bkA[:, 128:130], s_.SP[:], cind[:], True, True, [s_.SP.b, cind.b], [bkA.b])
                    act(s_.EQ[:], bBlo[:, 0:128], AF.Exp, [bBlo.b], [s_.EQ.b])
                    act(s_.EK[:], bBlo[:, 0:128], AF.Exp, [bBlo.b], [s_.EK.b], scale=-1.0)
                    act(s_.EE[:], bBlo[:, 128:256], AF.Exp, [bBlo.b], [s_.EE.b])
                    act(s_.DEC[:], bkA[:, 128:130], AF.Exp, [bkA.b], [s_.DEC.b])
                    tt("dve", s_.QE[:], q[:, cs], s_.EQ[:], ALU.mult, [q.b, s_.EQ.b], [s_.QE.b])
                    for h in range(4):
                        stt("dve", s_.KE[:, h, :], kk_[:, cs], hmask[:, h:h + 1], s_.EK[:],
                            ALU.mult, ALU.mult, [kk_.b, hmask.b, s_.EK.b], [s_.KE.b])
                    tt("dve", s_.KEND[:], tv[:, j, 0:128], s_.EE[:], ALU.mult, [tv.b, s_.EE.b], [s_.KEND.b])
                    for h in range(4):
                        mm(bkA[:, h * 128:(h + 1) * 128], s_.KE[:, h, :], s_.QE[:], True, True, [s_.KE.b, s_.QE.b], [bkA.b])
                    tt("dve", s_.AM[:].rearrange("p h n -> p (h n)"), bkA[:, :],
                       mask4[d_][:].rearrange("p h n -> p (h n)"), ALU.mult, [bkA.b, mask4[d_].b], [s_.AM.b])
                    return s_

                def SD(j, s_):
                    cs = slice(j * 128, (j + 1) * 128)
                    bo = bkC
                    for oc in range(2):
                        for hh in range(2):
                            h = 2 * oc + hh
                            mm(bo[:, oc * 128:(oc + 1) * 128], vp[:, j, h * 128:(h + 1) * 128], s_.AM[:, h, :],
                               hh == 0 and oc == 0, False, [vp.b, s_.AM.b], [bo.b])
                    for ci, c in enumerate(chunks):
                        for oc in range(2):
                            mm(bo[:, oc * 128 + c * 64:oc * 128 + (c + 1) * 64], g.Sb[:, oc * 128:(oc + 1) * 128],
                               s_.QE[:, c * 64:(c + 1) * 64], False, ci == 1 and oc == 1, [g.Sb.b, s_.QE.b], [bo.b])
                        mm(bBhi[:, 256:512], s_.KEND[c * 64:(c + 1) * 64, :], tv[c * 64:(c + 1) * 64, j, 128:384], True, True,
                           [s_.KEND.b, tv.b], [bBhi.b])
                        tt("dve", g.TMP[:], bBhi[:, 256:512], bmask[:].rearrange("p h n -> p (h n)"), ALU.mult,
                           [bBhi.b, bmask.b], [g.TMP.b])
                        stt("dve", g.S32[:], g.S32[:], s_.DEC[:, c:c + 1], g.TMP[:], ALU.mult, ALU.add,
                            [g.S32.b, s_.DEC.b, g.TMP.b], [g.S32.b])
                        cp("act", g.Sb[:], g.S32[:], [g.S32.b], [g.Sb.b])
                    cp("dve", so[:, :, cs], bo[:, 0:256].rearrange("p (o n) -> p o n", o=2), [bo.b], [so.b])

                prev = None
                for j in subs:
                    yield
                    cur = (j, SI(j))
                    if prev is not None:
                        yield
                        SD(*prev)
                    prev = cur
                yield
                SD(*prev)
                P.dma("act", OFD[d_, :, :, n0:n0 + w], so[:, :, :w], [so.b], [db("OFD%d" % d_, t)], stream="st")

            order_f = list(range(9))
            order_b = [0] + list(range(8, 0, -1))
            mg = None
            if l + 1 < nlay:
                ring = [sb("wada", [128, 2304], F32, es) for _ in range(3)]
                BM["mod"] = True
                mg = mod_gen(l + 1, ring, ps[1])
            def gla_all(d_, order):
                for it in range(9):
                    yield from gla_tile(d_, order[it], it)

            def conv_all():
                for it in range(9):
                    yield from conv_tile(it)

            gens = [[gla_all(0, order_f), 3], [gla_all(1, order_b), 3], [conv_all(), 1]]
            if mg is not None:
                gens.append([mg, 1])
            live = list(gens)
            while live:
                for ge in list(live):
                    for _ in range(ge[1]):
                        try:
                            next(ge[0])
                        except StopIteration:
                            if ge in live:
                                live.remove(ge)
                            break
            BM["mod"] = False
            of = [sb("of", [128, 2, 512], F32, es) for _ in range(2)]
            ob = [sb("ob", [128, 2, 512], F32, es) for _ in range(2)]
            gr = [sb("gr", [128, 2, 512], BF16, es) for _ in range(2)]
            sg = [sb("sg", [128, 2, 512], BF16, es) for _ in range(2)]
            for t, (n0, w) in enumerate(TILES):
                a, b2_, r_, y_ = of[t % 2], ob[t % 2], gr[t % 2], sg[t % 2]
                P.dma("sp", a[:, :, :w], OFD[0, :, :, n0:n0 + w], [db("OFD0", t)], [a.b])
                P.dma("sp", b2_[:, :, :w], OFD[1, :, :, n0:n0 + w], [db("OFD1", t)], [b2_.b])
                P.dma("sp", r_[:, :, :w], PGF[:, 2:4, n0:n0 + w], [db("PGF", t)], [r_.b])
                tt("dve", a[:, :, :w], a[:, :, :w], b2_[:, :, :w], ALU.add, [a.b, b2_.b], [a.b])
                for oc in range(2):
                    sq = ft()
                    tt("pool", sq[:, :w], a[:, oc, :w], a[:, oc, :w], ALU.mult, [a.b], [sq.b])
                    b1 = bank("st")
                    mm(b1[:, :w], blkf[:], sq[:, :w], True, True, [blkf.b, sq.b], [b1.b])
                    rs = ft()
                    rsqrt_act(rs[:, :w], b1[:, :w], [b1.b, epsT.b], [rs.b])
                    o2 = ft()
                    stt("dve", o2[:, :w], a[:, oc, :w], pa[:, 104 + oc:105 + oc], rs[:, :w], ALU.mult, ALU.mult,
                        [a.b, pa.b, rs.b], [o2.b])
                    tt("dve", y_[:, oc, :w], o2[:, :w], r_[:, oc, :w], ALU.mult, [o2.b, r_.b], [y_.b])
                P.dma("act", YCG[:, 2:4, n0:n0 + w], y_[:, :, :w], [y_.b], [db("YCG", t)], stream="st")
            P.barrier()

    sweep(None, 0)
    if stop == "pre":
        return finish(nc, P)
    for l in range(nlay):
        phase2a(l)
        if stop == "p2a%d" % l:
            return finish(nc, P)
        sweep(l, l + 1 if l + 1 < nlay else None)
        if stop == "sw%d" % l:
            return finish(nc, P)
    return finish(nc, P)


def finish(nc, P):
    P.barrier()
    P.emit()
    return nc, P


_CACHE = {}


def make_in_maps(inputs):
    f = lambda a: np.ascontiguousarray(np.asarray(a, dtype=np.float32))
    shared = {
        "c_ctx": f(inputs["c_ctx"]).reshape(8, 128),
        "w_ada": f(inputs["w_ada"]),
        "b_ada": f(inputs["b_ada"]).reshape(DEPTH, 72, 128),
        "g_norm": f(inputs["g_norm"]).reshape(DEPTH, 24, 128),
        "w_ffn_in": f(inputs["w_ffn_in"]),
        "w_ffn_out": f(inputs["w_ffn_out"]),
        "w_in": f(inputs["w_in"]),
        "w_out": f(inputs["w_out"]),
        "w_dw": f(inputs["w_dw"]).reshape(DEPTH, 62, 128),
        "b_dw": f(inputs["b_dw"]).reshape(DEPTH, 2, 128),
        "conv_norm_g": f(inputs["conv_norm_g"]).reshape(DEPTH, 2, 128),
        "conv_norm_b": f(inputs["conv_norm_b"]).reshape(DEPTH, 2, 128),
        "w_gla_gate": f(inputs["w_gla_gate"]),
        "b_gla_gate": f(inputs["b_gla_gate"]),
        "gla_norm_g": f(inputs["gla_norm_g"]).reshape(DEPTH, 2, 128),
        "q_norm_g": f(inputs["q_norm_g"]).reshape(DEPTH, 1, 64),
        "k_norm_g": f(inputs["k_norm_g"]).reshape(DEPTH, 1, 64),
    }
    x = f(inputs["x"])
    c = f(inputs["c"])
    ctx = f(inputs["ctx"])
    maps = []
    for b in range(8):
        m = dict(shared)
        m["x"] = x[b]
        m["c"] = c[b].reshape(8, 128)
        m["ctx"] = ctx[b]
        maps.append(m)
    return maps


def kernel(**inputs):
    if "nc" not in _CACHE:
        _CACHE["nc"] = build()[0]
    nc = _CACHE["nc"]
    res = run_bass_kernel_spmd(nc, make_in_maps(inputs), core_ids=list(range(8)))
    return np.stack([np.asarray(r["out"], dtype=np.float32) for r in res.results], axis=0)
```

```python
from contextlib import ExitStack
import math
import numpy as np
import concourse.bass as bass
import concourse.mybir as mybir
from concourse.bass_utils import run_bass_kernel_spmd

F32 = mybir.dt.float32
BF16 = mybir.dt.bfloat16
ALU = mybir.AluOpType
AF = mybir.ActivationFunctionType

D = 1024
SEQ = 4096
NCTX = 256
NT = SEQ + NCTX
DEPTH = 4
DFF = 2816
NJ = DFF // 128
INW = 2080
EPS = 1e-6
TILES = [(0, 256)] + [(256 + 512 * i, 512) for i in range(8)]
PCW = 15 + 256 + 15 + 15 + 4096 + 15


def pc_off(n):
    return 15 + n if n < 256 else n + 45


class Buf:
    __slots__ = ("name", "w", "r")

    def __init__(self, name):
        self.name = name
        self.w = None
        self.r = []


class Op:
    __slots__ = ("eng", "fn", "deps", "dma", "stream", "signal", "count", "sem", "epoch", "widx")


ENGS = ("pe", "act", "dve", "pool", "sp")
DMA_WIN = 6
POOL_WIN = False
ATT_RATIO = 1.75
ENGMAP = {"pe": "tensor", "act": "scalar", "dve": "vector", "pool": "gpsimd", "sp": "sync"}


class Prog:
    def __init__(self, nc):
        self.nc = nc
        self.ops = {e: [] for e in ENGS}
        self.all = []
        self.epoch = 0
        self.last = {}
        self.hist = {}

    def buf(self, name="b"):
        return Buf(name)

    def _mk(self, eng, fn, deps, dma, stream):
        op = Op()
        op.eng, op.fn, op.dma, op.stream = eng, fn, dma, stream
        op.signal = dma
        op.count = op.sem = None
        op.epoch = self.epoch
        op.widx = None
        if dma and (eng != "pool" or POOL_WIN):
            h = self.hist.setdefault(stream, [])
            op.widx = len(h)
            if len(h) >= DMA_WIN:
                deps = list(deps) + [h[len(h) - DMA_WIN]]
            h.append(op)
        seen = set()
        dd = []
        for d in deps:
            if id(d) in seen:
                continue
            seen.add(id(d))
            if (not dma) and (not d.dma) and d.eng == eng and eng == "pe":
                continue
            dd.append(d)
            d.signal = True
        op.deps = dd
        self.ops[eng].append(op)
        self.all.append(op)
        self.last[stream if dma else eng] = op
        return op

    def _add(self, eng, fn, reads, writes, dma=False, stream=None):
        deps = []
        for b in reads:
            if b.w is not None:
                deps.append(b.w)
        for b in writes:
            if b.w is not None:
                deps.append(b.w)
            deps.extend(b.r)
        op = self._mk(eng, fn, deps, dma, stream)
        for b in reads:
            b.r.append(op)
        for b in writes:
            b.w = op
            b.r = []
        return op

    def op(self, eng, fn, reads=(), writes=()):
        return self._add(eng, fn, list(reads), list(writes))

    def dma(self, eng, out, in_, reads=(), writes=(), stream=None):
        return self._add(eng, lambda e: e.dma_start(out=out, in_=in_), list(reads), list(writes),
                         dma=True, stream=stream or ("q_" + eng))

    def barrier(self):
        lasts = [o for k, o in self.last.items() if k not in self.hist]
        for h in self.hist.values():
            lasts.extend(h[-DMA_WIN:])
        for e in ENGS:
            self._mk(e, lambda eng: eng.nop(), lasts, False, None)
        self.epoch += 1

    def emit(self):
        nc = self.nc
        sems = {}
        counters = {}
        for op in self.all:
            if not op.signal:
                continue
            if op.widx is not None:
                key = (op.stream, 1000 + op.widx % DMA_WIN)
            else:
                key = ((op.stream if op.dma else op.eng), op.epoch)
            if key not in sems:
                sems[key] = nc.alloc_semaphore(name="s_%s_%d" % key)
            op.sem = sems[key]
            counters[key] = counters.get(key, 0) + (16 if op.dma else 1)
            op.count = counters[key]
        self.nsems = len(sems)
        self.maxcount = max(counters.values()) if counters else 0
        with nc.Block() as block:
            for ename in ENGS:
                oplist = self.ops[ename]
                if not oplist:
                    continue

                def body(eng, oplist=oplist):
                    waited = {}
                    for op in oplist:
                        need = {}
                        for d in op.deps:
                            k = id(d.sem)
                            if k not in need or need[k][1] < d.count:
                                need[k] = (d.sem, d.count)
                        for k, (sem, cnt) in need.items():
                            if waited.get(k, 0) >= cnt:
                                continue
                            eng.wait_ge(sem, cnt)
                            waited[k] = cnt
                        ins = op.fn(eng)
                        if op.signal:
                            ins.then_inc(op.sem, 16 if op.dma else 1)

                getattr(block, ENGMAP[ename])(body)


class TT:
    def __init__(self, t, b):
        self.t = t
        self.b = b

    def __getitem__(self, idx):
        return self.t[idx]


def build(nlay=DEPTH, dbg=False, stop=None):
    nc = bass.Bass("TRN2", target_bir_lowering=False)
    P = Prog(nc)
    uid = [0]

    def nm(s):
        uid[0] += 1
        return "%s_%d" % (s, uid[0])

    def dram(name, shape, dt, kind="Internal"):
        return nc.dram_tensor(name, shape, dt, kind=kind).ap()

    def sb(name, shape, dt, stack=None):
        if stack is None:
            t = nc.alloc_sbuf_tensor(nm(name), shape, dt)
        else:
            t = stack.enter_context(nc.sbuf_tensor(nm(name), shape, dt))
        return TT(t, P.buf(name))

    ein = lambda n, s: dram(n, s, F32, "ExternalInput")
    x_in = ein("x", [SEQ, D])
    c_in = ein("c", [8, 128])
    ctx_in = ein("ctx", [NCTX, D])
    cctx_in = ein("c_ctx", [8, 128])
    w_ada = ein("w_ada", [DEPTH, D, 9 * D])
    b_ada = ein("b_ada", [DEPTH, 72, 128])
    g_norm = ein("g_norm", [DEPTH, 24, 128])
    w_ffn_in = ein("w_ffn_in", [DEPTH, 2, D, 2 * DFF])
    w_ffn_out = ein("w_ffn_out", [DEPTH, 2, DFF, D])
    w_in = ein("w_in", [DEPTH, D, INW])
    w_out = ein("w_out", [DEPTH, D, D])
    w_dw = ein("w_dw", [DEPTH, 62, 128])
    b_dw = ein("b_dw", [DEPTH, 2, 128])
    cn_g = ein("conv_norm_g", [DEPTH, 2, 128])
    cn_b = ein("conv_norm_b", [DEPTH, 2, 128])
    w_gg = ein("w_gla_gate", [DEPTH, 2, 16, 128])
    b_gg = ein("b_gla_gate", [DEPTH, 2, 128])
    gla_g = ein("gla_norm_g", [DEPTH, 2, 128])
    qn_g = ein("q_norm_g", [DEPTH, 1, 64])
    kn_g = ein("k_norm_g", [DEPTH, 1, 64])
    out = dram("out", [SEQ, D], F32, "ExternalOutput")

    skind = "ExternalOutput" if dbg else "Internal"
    XT = dram("XT", [128, 8, NT], F32, skind)
    PQ = dram("PQ", [128, 4, NT], BF16, skind)
    PK = dram("PK", [128, NT], BF16, skind)
    PVA = dram("PVA", [128, 34, 128], BF16, skind)
    PVB = dram("PVB", [128, 34, 128], BF16, skind)
    PC = dram("PC", [128, 2, PCW], BF16, skind)
    PGF = dram("PGF", [128, 4, NT], BF16, skind)
    PG67 = dram("PG67", [32, NT], BF16, skind)
    PGT = dram("PGT", [128, 34, 384], BF16, skind)
    PGVP = dram("PGVP", [128, 34, 512], BF16, skind)
    OFD = dram("OFD", [2, 128, 2, NT], F32, skind)
    YCG = dram("YCG", [128, 4, NT], BF16, skind)
    COS = dram("COS", [128, NT], F32, skind)
    SIN = dram("SIN", [128, NT], F32, skind)
    W1s = [[dram("W1s_%d_%d" % (l, f), [11, 128, 8, 2, 256], BF16) for f in range(2)] for l in range(nlay)]
    W2s = [[dram("W2s_%d_%d" % (l, f), [8, 128, 22, 128], BF16) for f in range(2)] for l in range(nlay)]
    Wfm = [dram("Wfm_%d" % l, [128, 8, 1696], BF16) for l in range(nlay)]
    Wtm = [dram("Wtm_%d" % l, [128, 8, 512], BF16) for l in range(nlay)]
    Wos = [dram("Wos_%d" % l, [128, 8, 1024], BF16) for l in range(nlay)]
    dB = {}

    def db(name, t=0):
        k = (name, t)
        if k not in dB:
            dB[k] = P.buf("%s_%d" % (name, t))
        return dB[k]

    ps = [TT(nc.alloc_psum_tensor("ps%d" % i, [128, 512], F32), P.buf("ps%d" % i)) for i in range(8)]
    rr = {"wk": 0, "acc": 0, "st": 0, "as": 0}
    BM = {"split": False}

    def bank(kind):
        i = rr[kind]
        rr[kind] += 1
        if BM["split"]:
            if kind == "as":
                return ps[i % 2]
            if kind == "wk":
                return ps[4 + i % 2]
            return ps[6] if kind == "acc" else ps[7]
        if kind in ("wk", "as"):
            return ps[i % 4]
        if kind == "acc":
            return ps[4] if BM.get("mod") else ps[4 + i % 2]
        return ps[6 + i % 2]

    def mm(o, lhsT, rhs, start, stop, R, W):
        P.op("pe", lambda e: e.matmul(o, lhsT=lhsT, rhs=rhs, start=start, stop=stop), R, W)

    def act(o, i, func, R, W, eng="act", **kw):
        P.op(eng, lambda e: e.activation(out=o, in_=i, func=func, **kw), R, W)

    def tt(eng, o, a, b_, op, R, W):
        P.op(eng, lambda e: e.tensor_tensor(out=o, in0=a, in1=b_, op=op), R, W)

    def ts(eng, o, a, s1, s2, op0, op1, R, W):
        if op1 is None:
            P.op(eng, lambda e: e.tensor_scalar(out=o, in0=a, scalar1=s1, scalar2=None, op0=op0), R, W)
        else:
            P.op(eng, lambda e: e.tensor_scalar(out=o, in0=a, scalar1=s1, scalar2=s2, op0=op0, op1=op1), R, W)

    def stt(eng, o, a, s, b_, op0, op1, R, W):
        P.op(eng, lambda e: e.scalar_tensor_tensor(out=o, in0=a, scalar=s, in1=b_, op0=op0, op1=op1), R, W)

    def cp(eng, o, i, R, W):
        if eng == "act":
            P.op(eng, lambda e: e.copy(out=o, in_=i), R, W)
        else:
            P.op(eng, lambda e: e.tensor_copy(out=o, in_=i), R, W)

    def ms(eng, o, v, W):
        P.op(eng, lambda e: e.memset(o, v), [], W)

    def asel(o, i, pattern, cmp, base, cm, R, W, fill=0.0):
        P.op("pool", lambda e: e.affine_select(out=o, in_=i, pattern=pattern, compare_op=cmp, fill=fill,
                                               base=base, channel_multiplier=cm), R, W)

    def rsqrt_act(o, i, R, W):
        act(o, i, AF.Ln, R, W, bias=EPS_AP[:, 0:1])
        act(o, o, AF.Exp, W, W, scale=-0.5)

    ident = sb("ident", [128, 128], F32)
    ms("pool", ident[:], 1.0, [ident.b])
    asel(ident[:], ident[:], [[-1, 128]], ALU.is_equal, 0, 1, [ident.b], [ident.b])
    epsT = sb("eps", [128, 1], F32)
    ms("pool", epsT[:], EPS, [epsT.b])
    EPS_AP = epsT.t
    ones_mean = sb("ones_mean", [128, 128], BF16)
    ms("pool", ones_mean[:], 1.0 / 1024.0, [ones_mean.b])
    blkb = sb("blk64b", [128, 128], BF16)
    blkf = sb("blk64f", [128, 128], F32)
    for t_ in (blkb, blkf):
        ms("pool", t_[:], 1.0 / 64.0, [t_.b])
        ms("pool", t_[0:64, 64:128], 0.0, [t_.b])
        ms("pool", t_[64:128, 0:64], 0.0, [t_.b])
    d1 = sb("d1", [128, 128], F32)
    d2 = sb("d2", [128, 128], F32)
    rrot = sb("rrot", [128, 128], F32)
    ms("pool", d1[:], 1.0, [d1.b])
    asel(d1[:], d1[:], [[-1, 128]], ALU.is_equal, -16, 1, [d1.b], [d1.b])
    ms("pool", d1[:].rearrange("p (b h f) -> p b h f", b=4, h=2, f=16)[:, :, 1, :], 0.0, [d1.b])
    ms("pool", d2[:], 1.0, [d2.b])
    asel(d2[:], d2[:], [[-1, 128]], ALU.is_equal, 16, 1, [d2.b], [d2.b])
    ms("pool", d2[:].rearrange("p (b h f) -> p b h f", b=4, h=2, f=16)[:, :, 0, :], 0.0, [d2.b])
    tt("pool", rrot[:], d2[:], d1[:], ALU.subtract, [d1.b, d2.b], [rrot.b])
    Uc, Mend, mask4 = [], [], []
    for d_ in range(2):
        u = sb("Uc%d" % d_, [128, 128], F32)
        m_ = sb("Mend%d" % d_, [128, 128], F32)
        m4 = sb("mask4_%d" % d_, [128, 4, 128], F32)
        ms("pool", u[:], 1.0, [u.b])
        ms("pool", m_[:], 1.0, [m_.b])
        if d_ == 0:
            asel(u[:], u[:], [[1, 128]], ALU.is_ge, 0, -1, [u.b], [u.b])
            asel(m_[:], m_[:], [[-1, 128]], ALU.is_ge, -1, 1, [m_.b], [m_.b])
        else:
            asel(u[:], u[:], [[-1, 128]], ALU.is_ge, 0, 1, [u.b], [u.b])
            asel(m_[:], m_[:], [[1, 128]], ALU.is_ge, -1, -1, [m_.b], [m_.b])
        for t_ in (u, m_):
            ms("pool", t_[0:64, 64:128], 0.0, [t_.b])
            ms("pool", t_[64:128, 0:64], 0.0, [t_.b])
        for h in range(4):
            cp("pool", m4[:, h, :], u[:], [u.b], [m4.b])
        ts("pool", u[:], u[:], -1.0 / 16.0, None, ALU.mult, None, [u.b], [u.b])
        ts("pool", m_[:], m_[:], -1.0 / 16.0, None, ALU.mult, None, [m_.b], [m_.b])
        Uc.append(u)
        Mend.append(m_)
        mask4.append(m4)
    cind = sb("cind", [128, 2], F32)
    ms("pool", cind[:], -1.0 / 16.0, [cind.b])
    ms("pool", cind[0:64, 1:2], 0.0, [cind.b])
    ms("pool", cind[64:128, 0:1], 0.0, [cind.b])
    hmask = sb("hmask", [128, 4], F32)
    ms("pool", hmask[:], 1.0, [hmask.b])
    asel(hmask[:], hmask[:], [[-32, 4]], ALU.is_ge, 0, 1, [hmask.b], [hmask.b])
    asel(hmask[:], hmask[:], [[32, 4]], ALU.is_ge, 31, -1, [hmask.b], [hmask.b])
    bmask = sb("bmask", [128, 4, 64], F32)
    ms("pool", bmask[:], 1.0, [bmask.b])
    asel(bmask[:], bmask[:], [[-32, 4], [0, 64]], ALU.is_ge, 0, 1, [bmask.b], [bmask.b])
    asel(bmask[:], bmask[:], [[32, 4], [0, 64]], ALU.is_ge, 31, -1, [bmask.b], [bmask.b])

    WBd = {}

    def wbuf(l, name):
        k = (l, name)
        if k not in WBd:
            WBd[k] = P.buf("wc%d_%s" % (l % 3, name))
        return WBd[k]

    def cast_list(l):
        ops = []

        def cast(o, i, wb):
            ops.append(lambda: P.dma("pool", o, i, [], [wb], stream="cast_" + wb.name))

        def ffn_w(f):
            src = w_ffn_in[l, f].rearrange("(k p) n -> p k n", p=128)
            for g in range(11):
                for au in range(2):
                    c0 = au * DFF + g * 256
                    cast(W1s[l][f][g, :, :, au, :], src[:, :, c0:c0 + 256], wbuf(l, "w1_%d" % f))
            src2 = w_ffn_out[l, f].rearrange("(j p) n -> p j n", p=128)
            for m in range(8):
                cast(W2s[l][f][m], src2[:, :, m * 128:(m + 1) * 128], wbuf(l, "w2_%d" % f))

        ffn_w(0)
        src = w_in[l].rearrange("(k p) n -> p k n", p=128)
        wb = wbuf(l, "win")
        cast(Wfm[l][:, :, 0:768], src[:, :, 0:768], wb)
        cast(Wfm[l][:, :, 768:1024], src[:, :, 1024:1280], wb)
        for h in range(8):
            c0 = (8 + h % 4) * 128 + (h // 4) * 64
            cast(Wfm[l][:, :, c0:c0 + 64], src[:, :, 1312 + 64 * h:1312 + 64 * h + 64], wb)
        cast(Wfm[l][:, :, 1536:1664], src[:, :, 1824:1952], wb)
        cast(Wfm[l][:, :, 1664:1696], src[:, :, 1280:1312], wb)
        cast(Wtm[l][:, :, 0:384], src[:, :, 640:1024], wb)
        cast(Wtm[l][:, :, 384:512], src[:, :, 1952:2080], wb)
        wb = wbuf(l, "wout")
        cast(Wos[l][:, 0:4, :], w_out[l, 0:512, :].rearrange("(k p) n -> p k n", p=128), wb)
        for c_ in range(4):
            for hf in range(2):
                r0 = 512 + (c_ + 4 * hf) * 64
                cast(Wos[l][hf * 64:(hf + 1) * 64, 4 + c_, :], w_out[l, r0:r0 + 64, :], wb)
        ffn_w(1)
        return ops

    pending_casts = []

    def emit_casts(n):
        for _ in range(min(n, len(pending_casts))):
            pending_casts.pop(0)()


    PRMA = [sb("PRMA", [128, 108], F32) for _ in range(nlay)]
    PRMB = [sb("PRMB", [128, 62], F32) for _ in range(nlay)]
    MOD = [sb("MOD", [128, 72, 2], F32) for _ in range(nlay)]
    AA = [sb("AA", [128, 3, 8, 2], F32) for _ in range(nlay)]
    GH = [sb("GH", [128, 3, 8, 2], F32) for _ in range(nlay)]
    WG = [sb("wgb", [33, 2, 128], BF16) for _ in range(nlay)]
    S32 = sb("S32", [128, 16], F32)

    def mod_gen(l, ring, bm):
        pa = PRMA[l]
        wi = 0
        for k in range(8):
            for q in range(4):
                slot = ring[wi % len(ring)]
                wi += 1
                P.dma("sp", slot[:], w_ada[l, k * 128:(k + 1) * 128, q * 2304:(q + 1) * 2304], [], [slot.b])
                for n_ in range(18):
                    j = q * 18 + n_
                    mm(bm[:, 2 * j:2 * j + 2], slot[:, n_ * 128:(n_ + 1) * 128], S32[:, k:16:8],
                       k == 0 and j == 0, k == 7 and j == 71, [slot.b, S32.b], [bm.b])
                yield
        mod, aa, gh = MOD[l], AA[l], GH[l]
        for who in range(2):
            tt("dve", mod[:, :, who], bm[:, who:144:2], pa[:, 0:72], ALU.add, [bm.b, pa.b], [mod.b])
        for s in range(3):
            for who in range(2):
                stt("dve", aa[:, s, :, who], mod[:, (3 * s + 1) * 8:(3 * s + 2) * 8, who], 1.0,
                    pa[:, 72 + s * 8:72 + s * 8 + 8], ALU.add, ALU.mult, [mod.b, pa.b], [aa.b])
                ts("dve", gh[:, s, :, who], mod[:, (3 * s + 2) * 8:(3 * s + 3) * 8, who],
                   0.25 if s != 1 else 1.0, None, ALU.mult, None, [mod.b], [gh.b])
    with ExitStack() as st0:
        stc = sb("stc", [16, 128], F32, st0)
        P.dma("sp", stc[0:8, :], c_in, [], [stc.b])
        P.dma("sp", stc[8:16, :], cctx_in, [], [stc.b])
        b_ = bank("st")
        P.op("pe", lambda e: e.transpose(b_[:, 0:16], stc[0:16, :], ident[0:16, 0:16]), [stc.b, ident.b], [b_.b])
        act(S32[:], b_[:, 0:16], AF.Silu, [b_.b], [S32.b])
        wada_ring = [sb("wada%d" % i, [128, 2304], F32, st0) for i in range(3)]
        for l in range(nlay):
            sa = sb("stA", [128, 128], F32, st0)
            sbt = sb("stB", [64, 128], F32, st0)
            ld = lambda o, i, t_: P.dma("sp", o, i, [], [t_.b])
            ld(sa[0:72, :], b_ada[l], sa)
            ld(sa[72:96, :], g_norm[l], sa)
            ld(sa[96:98, :], b_dw[l], sa)
            ld(sa[98:100, :], cn_g[l], sa)
            ld(sa[100:102, :], cn_b[l], sa)
            ld(sa[102:104, :], b_gg[l], sa)
            ld(sa[104:106, :], gla_g[l], sa)
            ld(sa[106:107, 0:64], qn_g[l], sa)
            ld(sa[106:107, 64:128], qn_g[l], sa)
            ld(sa[107:108, 0:64], kn_g[l], sa)
            ld(sa[107:108, 64:128], kn_g[l], sa)
            ld(sbt[0:62, :], w_dw[l], sbt)
            pa = PRMA[l]
            pb = PRMB[l]
            b1 = bank("st")
            P.op("pe", lambda e, b1=b1, sa=sa: e.transpose(b1[:, 0:108], sa[0:108, :], ident[0:108, 0:108]),
                 [sa.b, ident.b], [b1.b])
            cp("dve", pa[:], b1[:, 0:108], [b1.b], [pa.b])
            b2 = bank("st")
            P.op("pe", lambda e, b2=b2, sbt=sbt: e.transpose(b2[:, 0:62], sbt[0:62, :], ident[0:62, 0:62]),
                 [sbt.b, ident.b], [b2.b])
            cp("dve", pb[:], b2[:, 0:62], [b2.b], [pb.b])
            wg32 = sb("wg32", [33, 2, 128], F32, st0)
            wgb = WG[l]
            ms("dve", wg32[:], 0.0, [wg32.b])
            P.dma("sp", wg32[0:16, 0, :], w_gg[l, 0], [], [wg32.b])
            P.dma("sp", wg32[16:32, 1, :], w_gg[l, 1], [], [wg32.b])
            P.dma("sp", wg32[32:33, :, :], b_gg[l:l + 1], [], [wg32.b])
            cp("dve", wgb[:], wg32[:], [wg32.b], [wgb.b])
            if l == 0:
                for _ in mod_gen(0, wada_ring, bank("acc")):
                    pass

        I32 = mybir.dt.int32
        pidx = sb("pidx", [128, 1], F32, st0)
        P.op("pool", lambda e: e.iota(pidx[:], pattern=[[0, 1]], base=0, channel_multiplier=1,
                                      allow_small_or_imprecise_dtypes=True), [], [pidx.b])
        freq = sb("freq", [128, 1], F32, st0)
        axm = sb("axm", [128, 1], F32, st0)
        pti = sb("pti", [128, 1], I32, st0)
        ptf = sb("ptf", [128, 1], F32, st0)

        def floordiv(dst, src, div, ti):
            ts("dve", dst[:], src[:], -(div - 1) / 2.0, 1.0 / div, ALU.add, ALU.mult, [src.b], [dst.b])
            cp("dve", ti[:], dst[:], [dst.b], [ti.b])
            cp("dve", dst[:], ti[:], [ti.b], [dst.b])

        floordiv(ptf, pidx, 16.0, pti)
        stt("dve", freq[:], ptf[:], -16.0, pidx[:], ALU.mult, ALU.add, [ptf.b, pidx.b], [freq.b])
        act(freq[:], freq[:], AF.Exp, [freq.b], [freq.b], scale=-math.log(10000.0) / 16.0)
        floordiv(ptf, pidx, 64.0, pti)
        stt("dve", axm[:], ptf[:], -64.0, pidx[:], ALU.mult, ALU.add, [ptf.b, pidx.b], [axm.b])
        ts("dve", axm[:], axm[:], 32.0, None, ALU.is_ge, None, [axm.b], [axm.b])
        tok = sb("tok", [128, SEQ], F32, st0)
        colp = sb("colp", [128, SEQ], F32, st0)
        rowp = sb("rowp", [128, SEQ], F32, st0)
        toki = sb("toki", [128, SEQ], I32, st0)
        P.op("pool", lambda e: e.iota(tok[:], pattern=[[1, SEQ]], base=0, channel_multiplier=0,
                                      allow_small_or_imprecise_dtypes=True), [], [tok.b])
        pending_casts.extend(cast_list(0))
        emit_casts(1000)
        floordiv(rowp, tok, 64.0, toki)
        stt("dve", colp[:], rowp[:], -64.0, tok[:], ALU.mult, ALU.add, [rowp.b, tok.b], [colp.b])
        tt("dve", colp[:], colp[:], rowp[:], ALU.subtract, [colp.b, rowp.b], [colp.b])
        stt("dve", tok[:], colp[:], axm[:, 0:1], rowp[:], ALU.mult, ALU.add, [colp.b, rowp.b, axm.b], [tok.b])
        ts("dve", tok[:], tok[:], freq[:, 0:1], None, ALU.mult, None, [tok.b, freq.b], [tok.b])
        PI = math.pi

        def sin_of(dst, shift, dram_dst):
            ts("dve", rowp[:], tok[:], shift, None, ALU.add, None, [tok.b], [rowp.b])
            ts("dve", dst[:], rowp[:], 1.0 / (2 * PI), None, ALU.mult, None, [rowp.b], [dst.b])
            cp("dve", toki[:], dst[:], [dst.b], [toki.b])
            cp("dve", dst[:], toki[:], [toki.b], [dst.b])
            stt("dve", dst[:], dst[:], -2 * PI, rowp[:], ALU.mult, ALU.add, [dst.b, rowp.b], [dst.b])
            ts("dve", dst[:], dst[:], PI, -PI, ALU.min, ALU.max, [dst.b], [dst.b])
            act(dst[:], dst[:], AF.Sin, [dst.b], [dst.b])
            P.dma("sp", dram_dst, dst[:], [dst.b], [db("COS")])

        sin_of(colp, 0.0, SIN[:, NCTX:NT])
        sin_of(colp, 0.5 * PI, COS[:, NCTX:NT])
        one_t = sb("one_t", [128, NCTX], F32, st0)
        zero_t = sb("zero_t", [128, 2, 32], BF16, st0)
        ms("dve", one_t[:], 1.0, [one_t.b])
        P.dma("sp", COS[:, 0:NCTX], one_t[:], [one_t.b], [db("COS")])
        zt2 = sb("zt2", [128, NCTX], F32, st0)
        ms("dve", zt2[:], 0.0, [zt2.b])
        P.dma("sp", SIN[:, 0:NCTX], zt2[:], [zt2.b], [db("COS")])
        ms("dve", zero_t[:], 0.0, [zero_t.b])
        for c0 in (0, 15 + 256, 15 + 256 + 30 + 4096):
            wdt = 15 if c0 != 15 + 256 else 30
            P.dma("sp", PC[:, :, c0:c0 + wdt], zero_t[:, :, 0:wdt], [zero_t.b], [db("PChalo")])
        P.barrier()

    if dbg:
        for nm_, t_, shp in (("D_PRMA", PRMA[0], [128, 108]), ("D_MOD", MOD[0], [128, 144]), ("D_AA", AA[0], [128, 48]),
                             ("D_GH", GH[0], [128, 48])):
            dd_ = dram(nm_, shp, F32, "ExternalOutput")
            src_ = t_[:] if len(t_.t.shape) == 2 else t_[:].rearrange("p a b -> p (a b)") if len(t_.t.shape) == 3 \
                else t_[:].rearrange("p a b c -> p (a b c)")
            P.dma("sp", dd_, src_, [t_.b], [db(nm_)])
    if stop == "setup":
        return finish(nc, P)

    def prenorm(S, X, w, l, s, who):
        act(S.SQ[:, :, :w], X[:, :, :w], AF.Square, [X.b], [S.SQ.b])
        b_ = bank("st")
        for k in range(8):
            mm(b_[:, :w], ones_mean[:], S.SQ[:, k, :w], k == 0, k == 7, [ones_mean.b, S.SQ.b], [b_.b])
        rsqrt_act(S.RSTD[:, :w], b_[:, :w], [b_.b, epsT.b], [S.RSTD.b])
        for k in range(8):
            tmp = S.tmp()
            stt("dve", tmp[:, :w], X[:, k, :w], AA[l][:, s, k, who:who + 1], S.RSTD[:, :w], ALU.mult, ALU.mult,
                [X.b, AA[l].b, S.RSTD.b], [tmp.b])
            ts("dve", S.Hn[:, k, :w], tmp[:, :w], MOD[l][:, 3 * s * 8 + k, who:who + 1], None, ALU.add, None,
               [tmp.b, MOD[l].b], [S.Hn.b])

    def ffn(S, X, w, l, f, s, who):
        prenorm(S, X, w, l, s, who)
        yield
        for g in range(11):
            slot = S.wslot()
            P.dma("sp", slot[:], W1s[l][f][g].rearrange("p k a c -> p (k a c)"), [wbuf(l, "w1_%d" % f)], [slot.b], stream="wld")
            sv = slot[:].rearrange("p (k a c) -> p k a c", k=8, a=2, c=256)
            for jj in range(2):
                j = 2 * g + jj
                pa_, pu_ = bank("wk"), bank("wk")
                for k in range(8):
                    mm(pa_[:, :w], sv[:, k, 0, jj * 128:(jj + 1) * 128], S.Hn[:, k, :w], k == 0, k == 7,
                       [slot.b, S.Hn.b], [pa_.b])
                for k in range(8):
                    mm(pu_[:, :w], sv[:, k, 1, jj * 128:(jj + 1) * 128], S.Hn[:, k, :w], k == 0, k == 7,
                       [slot.b, S.Hn.b], [pu_.b])
                tmp = S.tmp()
                act(tmp[:, :w], pa_[:, :w], AF.Tanh, [pa_.b], [tmp.b], scale=0.5)
                tmp2 = S.tmp()
                stt("dve", tmp2[:, :w], tmp[:, :w], 1.0, pa_[:, :w], ALU.add, ALU.mult, [tmp.b, pa_.b], [tmp2.b])
                tt("dve", S.ACT[:, j, :w], tmp2[:, :w], pu_[:, :w], ALU.mult, [tmp2.b, pu_.b], [S.ACT.b])
                yield
        for m in range(8):
            slot = S.w2slot()
            P.dma("sp", slot[:], W2s[l][f][m], [wbuf(l, "w2_%d" % f)], [slot.b], stream="wld")
            po = bank("acc")
            for j in range(NJ):
                mm(po[:, :w], slot[:, j, :], S.ACT[:, j, :w], j == 0, j == NJ - 1, [slot.b, S.ACT.b], [po.b])
            stt("dve", X[:, m, :w], po[:, :w], GH[l][:, s, m, who:who + 1], X[:, m, :w], ALU.mult, ALU.add,
                [po.b, GH[l].b, X.b], [X.b])
            yield

    def headnorm_rope(S, pb_, w, gcol, l, dst, dstb, n0):
        sq = S.tmpb()
        act(sq[:, :w], pb_[:, :w], AF.Square, [pb_.b], [sq.b])
        b2 = bank("st")
        mm(b2[:, :w], blkb[:], sq[:, :w], True, True, [blkb.b, sq.b], [b2.b])
        rs = S.tmp()
        rsqrt_act(rs[:, :w], b2[:, :w], [b2.b, epsT.b], [rs.b])
        qn = S.tmp()
        stt("dve", qn[:, :w], pb_[:, :w], PRMA[l][:, gcol:gcol + 1], rs[:, :w], ALU.mult, ALU.mult,
            [pb_.b, PRMA[l].b, rs.b], [qn.b])
        b3 = bank("st")
        mm(b3[:, :w], rrot[:], qn[:, :w], True, True, [rrot.b, qn.b], [b3.b])
        t1 = S.tmp()
        tt("pool", t1[:, :w], qn[:, :w], S.COS[:, :w], ALU.mult, [qn.b, S.COS.b], [t1.b])
        t2 = S.tmp()
        tt("dve", t2[:, :w], b3[:, :w], S.SIN[:, :w], ALU.mult, [b3.b, S.SIN.b], [t2.b])
        tt("pool", dst, t1[:, :w], t2[:, :w], ALU.add, [t1.b, t2.b], [dstb])

    def project(S, X, t, n0, w, l, who):
        prenorm(S, X, w, l, 1, who)
        yield
        nsub = w // 128
        kc0 = n0 // 128
        Hn = S.Hn

        def piece(c0, ncol, src):
            slot = S.wslot()
            P.dma("sp", slot[:, 0:8 * ncol].rearrange("p (k c) -> p k c", k=8), src[:, :, c0:c0 + ncol],
                  [wbuf(l, "win")], [slot.b], stream="wld")
            return slot, slot[:, 0:8 * ncol].rearrange("p (k c) -> p k c", k=8)

        def proj(sv, slot, c, m=128):
            pb_ = bank("wk")
            for k in range(8):
                mm(pb_[0:m, :w], sv[:, k, c * 128:c * 128 + m], Hn[:, k, :w], k == 0, k == 7, [slot.b, Hn.b], [pb_.b])
            return pb_

        def gated(pg_, pa_, dst, dstb):
            tmp = S.tmp()
            act(tmp[:, :w], pg_[:, :w], AF.Tanh, [pg_.b], [tmp.b], scale=0.5)
            tmp2 = S.tmp()
            src = pa_ if pa_ is not None else pg_
            stt("dve", tmp2[:, :w], tmp[:, :w], 1.0, src[:, :w], ALU.add, ALU.mult, [tmp.b, src.b], [tmp2.b])
            ts("dve", dst, tmp2[:, :w], 0.5, None, ALU.mult, None, [tmp2.b], [dstb])

        P.dma("sp", S.COS[:, :w], COS[:, n0:n0 + w], [db("COS")], [S.COS.b])
        P.dma("sp", S.SIN[:, :w], SIN[:, n0:n0 + w], [db("COS")], [S.SIN.b])
        slot, sv = piece(1024, 512, Wfm[l])
        for c in range(4):
            pb_ = proj(sv, slot, c)
            headnorm_rope(S, pb_, w, 106, l, S.stQ[:, c, :w], S.stQ.b, n0)
            yield
        P.dma("act", PQ[:, :, n0:n0 + w], S.stQ[:, :, :w], [S.stQ.b], [db("PQ", t)], stream="st")
        slot, sv = piece(1536, 160, Wfm[l])
        pb_ = proj(sv, slot, 0)
        headnorm_rope(S, pb_, w, 107, l, S.stK[:, :w], S.stK.b, n0)
        P.dma("act", PK[:, n0:n0 + w], S.stK[:, :w], [S.stK.b], [db("PK", t)], stream="st")
        pb_ = proj(sv, slot, 1, m=32)
        cp("dve", S.st67[0:32, :w], pb_[0:32, :w], [pb_.b], [S.st67.b])
        P.dma("act", PG67[:, n0:n0 + w], S.st67[0:32, :w], [S.st67.b], [db("PG67", t)], stream="st")
        yield
        slot, sv = piece(0, 512, Wfm[l])
        for cc in range(2):
            pa_ = proj(sv, slot, cc)
            pg_ = proj(sv, slot, 2 + cc)
            gated(pg_, pa_, S.stC[:, cc, :w], S.stC.b)
            yield
        P.dma("act", PC[:, :, pc_off(n0):pc_off(n0) + w], S.stC[:, :, :w], [S.stC.b], [db("PC", t)], stream="st")
        slot, sv = piece(512, 512, Wfm[l])
        pb_ = proj(sv, slot, 0)
        act(S.stG[:, 0, :w], pb_[:, :w], AF.Identity, [pb_.b], [S.stG.b], scale=32.0 ** -0.5)
        pb_ = proj(sv, slot, 1)
        cp("dve", S.stG[:, 1, :w], pb_[:, :w], [pb_.b], [S.stG.b])
        yield
        for cc in range(2):
            pb_ = proj(sv, slot, 2 + cc)
            gated(pb_, None, S.stG[:, 2 + cc, :w], S.stG.b)
            yield
        P.dma("act", PGF[:, :, n0:n0 + w], S.stG[:, :, :w], [S.stG.b], [db("PGF", t)], stream="st")
        slot, sv = piece(0, 512, Wtm[l])
        for j in range(nsub):
            pb_ = bank("wk")
            for k in range(8):
                mm(pb_[:, :], Hn[:, k, j * 128:(j + 1) * 128], sv[:, k, :], k == 0, k == 7, [slot.b, Hn.b], [pb_.b])
            cp("dve", S.stT[:, j, :], pb_[:, 0:384], [pb_.b], [S.stT.b])
            vp = S.stVP[:, j, :].rearrange("p (h two d) -> p h two d", h=4, two=2, d=64)
            src = pb_[:, 128:384].rearrange("p (h d) -> p h d", h=4, d=64)
            cp("dve", vp[:, 0:4:2, 0, :], src[:, 0:4:2, :], [pb_.b], [S.stVP.b])
            cp("act", vp[:, 1:4:2, 1, :], src[:, 1:4:2, :], [pb_.b], [S.stVP.b])
            cp("act", S.stVA[:, j, 0:64], pb_[:, 384:448], [pb_.b], [S.stVA.b])
            cp("dve", S.stVB[:, j, 64:128], pb_[:, 448:512], [pb_.b], [S.stVB.b])
            yield
        P.dma("act", PGT[:, kc0:kc0 + nsub, :], S.stT[:, 0:nsub, :], [S.stT.b], [db("PGT", t)], stream="st")
        P.dma("act", PGVP[:, kc0:kc0 + nsub, :], S.stVP[:, 0:nsub, :], [S.stVP.b], [db("PGVP", t)], stream="st")
        P.dma("act", PVA[:, kc0:kc0 + nsub, :], S.stVA[:, 0:nsub, :], [S.stVA.b], [db("PVA", t)], stream="st")
        P.dma("act", PVB[:, kc0:kc0 + nsub, :], S.stVB[:, 0:nsub, :], [S.stVB.b], [db("PVB", t)], stream="st")

    def attention(S, t, n0, w):
        P.dma("sp", S.Q[:, :, :w], PQ[:, :, n0:n0 + w], [db("PQ", t)], [S.Q.b])
        kcs = list(range(2)) if t == 0 else list(range(34))
        OA, OB = (ps[2], ps[3]) if BM["split"] else (ps[4], ps[5])
        for c in range(4):
            pend = None
            for idx, kc in enumerate(kcs):
                sA, sB = bank("as"), bank("as")
                mm(sA[:, :w], S.KT[0:64, kc * 128:(kc + 1) * 128], S.Q[0:64, c, :w], True, True, [S.KT.b, S.Q.b], [sA.b])
                mm(sB[:, :w], S.KT[64:128, kc * 128:(kc + 1) * 128], S.Q[64:128, c, :w], True, True,
                   [S.KT.b, S.Q.b], [sB.b])
                pA, pB = S.pt(), S.pt()
                act(pA[:, :w], sA[:, :w], AF.Exp, [sA.b], [pA.b], scale=0.125)
                act(pB[:, :w], sB[:, :w], AF.Exp, [sB.b], [pB.b], scale=0.125)
                if pend is not None:
                    pend()

                def pv(kc=kc, pA=pA, pB=pB, first=(idx == 0), last=(idx == len(kcs) - 1)):
                    mm(OA[:, :w], S.VA[:, kc, :], pA[:, :w], first, last, [S.VA.b, pA.b], [OA.b])
                    mm(OB[:, :w], S.VB[:, kc, :], pB[:, :w], first, last, [S.VB.b, pB.b], [OB.b])
                pend = pv
                yield
            pend()
            rc = S.tmp()
            P.op("dve", lambda e, rc=rc: e.reciprocal(out=rc[0:64, :w], in_=OA[64:128, :w]), [OA.b], [rc.b])
            P.op("dve", lambda e, rc=rc: e.reciprocal(out=rc[64:128, :w], in_=OB[0:64, :w]), [OB.b], [rc.b])
            tt("dve", S.YA[0:64, c, :w], OA[0:64, :w], rc[0:64, :w], ALU.mult, [OA.b, rc.b], [S.YA.b])
            tt("dve", S.YA[64:128, c, :w], OB[64:128, :w], rc[64:128, :w], ALU.mult, [OB.b, rc.b], [S.YA.b])
            yield

    def wout(S, X, t, n0, w, l, who):
        s1, s2 = S.wslot(), S.wslot()
        P.dma("sp", s1[:].rearrange("p (k c) -> p k c", k=4), Wos[l][:, 0:4, :], [wbuf(l, "wout")], [s1.b], stream="wld")
        P.dma("sp", s2[:].rearrange("p (k c) -> p k c", k=4), Wos[l][:, 4:8, :], [wbuf(l, "wout")], [s2.b], stream="wld")
        v1 = s1[:].rearrange("p (k c) -> p k c", k=4)
        v2 = s2[:].rearrange("p (k c) -> p k c", k=4)
        for m in range(8):
            po = bank("acc")
            for k in range(8):
                sv, sl = (v1, s1) if k < 4 else (v2, s2)
                rhs, rb = (S.YT[:, k, :w], S.YT.b) if k < 4 else (S.YA[:, k - 4, :w], S.YA.b)
                mm(po[:, :w], sv[:, k % 4, m * 128:(m + 1) * 128], rhs, k == 0, k == 7, [sl.b, rb], [po.b])
            stt("dve", X[:, m, :w], po[:, :w], GH[l][:, 1, m, who:who + 1], X[:, m, :w], ALU.mult, ALU.add,
                [po.b, GH[l].b, X.b], [X.b])
            yield

    class SweepBufs:
        pass

    def drain(g):
        for _ in g:
            pass

    def interleave(main, side, ratio):
        acc_ = 0.0
        side_done = side is None
        for _ in main:
            acc_ += ratio
            while not side_done and acc_ >= 1.0:
                acc_ -= 1.0
                try:
                    next(side)
                except StopIteration:
                    side_done = True
        if not side_done:
            drain(side)

    def sweep(l_mix, l_next):
        BM["split"] = l_mix is not None
        with ExitStack() as es:
            S = SweepBufs()
            Xb = [sb("X", [128, 8, 512], F32, es) for _ in range(2)]
            S.Hn = sb("Hn", [128, 8, 512], BF16, es)
            S.ACT = sb("ACT", [128, NJ, 512], BF16, es)
            S.SQ = TT(S.ACT.t[:, 0:8, :], S.ACT.b)
            S.RSTD = sb("RSTD", [128, 512], F32, es)
            tmps = [sb("tmp", [128, 512], F32, es) for _ in range(5)]
            tmpbs = [sb("tmpb", [128, 512], BF16, es) for _ in range(2)]
            wring = [sb("wr", [128, 4096], BF16, es) for _ in range(3)]
            w2ring = [sb("w2r", [128, NJ, 128], BF16, es) for _ in range(3)]
            cnt = {"t": 0, "tb": 0, "w": 0, "w2": 0, "pt": 0}

            def rot(lst, key):
                def f():
                    cnt[key] += 1
                    return lst[cnt[key] % len(lst)]
                return f
            S.tmp, S.tmpb, S.wslot, S.w2slot = rot(tmps, "t"), rot(tmpbs, "tb"), rot(wring, "w"), rot(w2ring, "w2")
            if l_mix is not None:
                S.YT = sb("YT", [128, 4, 512], BF16, es)
                S.YA = sb("YA", [128, 4, 512], BF16, es)
                S.Q = sb("Q", [128, 4, 512], BF16, es)
                S.KT = sb("KT", [128, NT], BF16, es)
                S.VA = sb("VA", [128, 34, 128], BF16, es)
                S.VB = sb("VB", [128, 34, 128], BF16, es)
                pts = [sb("pt", [128, 512], BF16, es) for _ in range(4)]
                S.pt = rot(pts, "pt")
                P.dma("sp", S.KT[:], PK, [db("PK", t) for t in range(9)], [S.KT.b])
                P.dma("sp", S.VA[:], PVA, [db("PVA", t) for t in range(9)], [S.VA.b])
                P.dma("sp", S.VB[:], PVB, [db("PVB", t) for t in range(9)], [S.VB.b])
            if l_next is not None:
                S.COS = sb("COS", [128, 512], F32, es)
                S.SIN = sb("SIN", [128, 512], F32, es)
                S.stC = sb("stC", [128, 2, 512], BF16, es)
                S.stG = sb("stG", [128, 4, 512], BF16, es)
                S.stQ = sb("stQ", [128, 4, 512], BF16, es)
                S.stK = sb("stK", [128, 512], BF16, es)
                S.st67 = sb("st67", [32, 512], BF16, es)
                S.stT = sb("stT", [128, 4, 384], BF16, es)
                S.stVP = sb("stVP", [128, 4, 512], BF16, es)
                S.stVA = sb("stVA", [128, 4, 128], BF16, es)
                S.stVB = sb("stVB", [128, 4, 128], BF16, es)
                ms("pool", S.stVP[:], 0.0, [S.stVP.b])
                ms("pool", S.stVA[:], 1.0, [S.stVA.b])
                ms("pool", S.stVB[:], 1.0, [S.stVB.b])
            if l_mix is None or l_next is None:
                S.xio = [sb("xio", [128, 1024], F32, es) for _ in range(2)]
            if l_next is not None and l_next + 1 < nlay and l_mix is not None:
                pending_casts.extend(cast_list(l_next + 1))
                emit_casts(1000)
            if l_mix is None and nlay > 1:
                pending_casts.extend(cast_list(1))
            nxc = {"n": 0}
            tl = [t for t in range(9) if not (l_next is None and t == 0)]

            def load_x(i):
                t = tl[i]
                n0, w = TILES[t]
                X = Xb[i % 2]
                P.dma("sp", X[:, :, :w], XT[:, :, n0:n0 + w], [db("XT", t)], [X.b])

            def load_y(i):
                t = tl[i]
                n0, w = TILES[t]
                P.dma("sp", S.YT[:, :, :w], YCG[:, :, n0:n0 + w], [db("YCG", t)], [S.YT.b])

            def main_gen(i):
                t = tl[i]
                n0, w = TILES[t]
                who = 1 if t == 0 else 0
                nsub = w // 128
                X = Xb[i % 2]
                emit_casts(11)
                if l_mix is None:
                    for j in range(nsub):
                        xi = S.xio[nxc["n"] % 2]
                        nxc["n"] += 1
                        src = ctx_in[j * 128:(j + 1) * 128, :] if t == 0 else \
                            x_in[n0 - NCTX + j * 128:n0 - NCTX + (j + 1) * 128, :]
                        P.dma("sp", xi[:], src, [], [xi.b])
                        for kh in range(2):
                            b_ = bank("wk")
                            for kk in range(4):
                                k = kh * 4 + kk
                                P.op("pe", lambda e, b_=b_, xi=xi, k=k, kk=kk: e.transpose(
                                    b_[:, kk * 128:(kk + 1) * 128], xi[:, k * 128:(k + 1) * 128], ident[:]),
                                    [xi.b, ident.b], [b_.b])
                            cp("dve" if kh == 0 else "act", X[:, kh * 4:(kh + 1) * 4, j * 128:(j + 1) * 128],
                               b_[:, :].rearrange("p (k n) -> p k n", k=4), [b_.b], [X.b])
                else:
                    yield from wout(S, X, t, n0, w, l_mix, who)
                    if i + 1 < len(tl):
                        load_x(i + 1)
                        load_y(i + 1)
                    yield from ffn(S, X, w, l_mix, 1, 2, who)
                if l_next is not None:
                    yield from ffn(S, X, w, l_next, 0, 0, who)
                    yield from project(S, X, t, n0, w, l_next, who)
                    P.dma("act", XT[:, :, n0:n0 + w], X[:, :, :w], [X.b], [db("XT", t)], stream="st")
                else:
                    for j in range(nsub):
                        xo = S.xio[nxc["n"] % 2]
                        nxc["n"] += 1
                        for kh in range(2):
                            b_ = bank("wk")
                            for kk in range(4):
                                k = kh * 4 + kk
                                P.op("pe", lambda e, b_=b_, k=k, kk=kk, j=j, X=X: e.transpose(
                                    b_[:, kk * 128:(kk + 1) * 128], X[:, k, j * 128:(j + 1) * 128], ident[:]),
                                    [X.b, ident.b], [b_.b])
                            cp("dve" if kh == 0 else "act", xo[:, kh * 512:(kh + 1) * 512], b_[:, :], [b_.b], [xo.b])
                        r0 = n0 - NCTX + j * 128
                        P.dma("act", out[r0:r0 + 128, :], xo[:], [xo.b], [db("out")], stream="st")
                        yield

            if l_mix is not None:
                load_x(0)
                load_y(0)
                drain(attention(S, tl[0], *TILES[tl[0]]))
            for i in range(len(tl)):
                if i == len(tl) - 1:
                    emit_casts(1000)
                side = None
                if l_mix is not None and i + 1 < len(tl):
                    side = attention(S, tl[i + 1], *TILES[tl[i + 1]])
                interleave(main_gen(i), side, 142.0 / (83 if l_next is not None else 42))
            P.barrier()
        BM["split"] = False

    def phase2a(l):
        with ExitStack() as es:
            pa, pb = PRMA[l], PRMB[l]
            Dg = sb("Dg", [128, 62, 128], BF16, es)
            for i in range(62):
                ts("pool" if i % 2 else "dve", Dg[:, i, :], ident[:], pb[:, i:i + 1], None, ALU.mult, None,
                   [ident.b, pb.b], [Dg.b])
            f32t = [sb("c32", [128, 512], F32, es) for _ in range(6)]
            cc_ = {"i": 0}

            def ft():
                cc_["i"] += 1
                return f32t[cc_["i"] % len(f32t)]
            hp = [sb("hpad", [128, 2, 542], BF16, es) for _ in range(2)]
            stY = [sb("stY", [128, 2, 512], BF16, es) for _ in range(2)]

            def conv_tile(t):
                yield
                n0, w = TILES[t]
                h = hp[t % 2]
                rd = [db("PC", tt_) for tt_ in range(max(0, t - 1), min(8, t + 1) + 1)] + [db("PChalo")]
                P.dma("sp", h[:, :, 0:w + 30], PC[:, :, pc_off(n0) - 15:pc_off(n0) + w + 15], rd, [h.b])
                sy = stY[t % 2]
                for cc in range(2):
                    b_ = ps[0]
                    for k in range(31):
                        mm(b_[:, :w], Dg[:, 2 * k + cc, :], h[:, cc, k:k + w], k == 0, k == 30, [Dg.b, h.b], [b_.b])
                    yield
                    cv = ft()
                    act(cv[:, :w], b_[:, :w], AF.Identity, [b_.b, pa.b], [cv.b], bias=pa[:, 96 + cc:97 + cc])
                    b1 = ps[0]
                    mm(b1[:, :w], blkf[:], cv[:, :w], True, True, [blkf.b, cv.b], [b1.b])
                    dd = ft()
                    tt("dve", dd[:, :w], cv[:, :w], b1[:, :w], ALU.subtract, [cv.b, b1.b], [dd.b])
                    dq = ft()
                    tt("pool", dq[:, :w], dd[:, :w], dd[:, :w], ALU.mult, [dd.b], [dq.b])
                    b2 = ps[0]
                    mm(b2[:, :w], blkf[:], dq[:, :w], True, True, [blkf.b, dq.b], [b2.b])
                    rs = ft()
                    rsqrt_act(rs[:, :w], b2[:, :w], [b2.b, epsT.b], [rs.b])
                    tt("dve", dd[:, :w], dd[:, :w], rs[:, :w], ALU.mult, [dd.b, rs.b], [dd.b])
                    ts("pool", dd[:, :w], dd[:, :w], pa[:, 98 + cc:99 + cc], pa[:, 100 + cc:101 + cc], ALU.mult, ALU.add,
                       [dd.b, pa.b], [dd.b])
                    yield
                    th = ft()
                    act(th[:, :w], dd[:, :w], AF.Tanh, [dd.b], [th.b], scale=0.5)
                    stt("dve", th[:, :w], th[:, :w], 1.0, dd[:, :w], ALU.add, ALU.mult, [th.b, dd.b], [th.b])
                    ts("dve", sy[:, cc, :w], th[:, :w], 0.5, None, ALU.mult, None, [th.b], [sy.b])
                P.dma("act", YCG[:, 0:2, n0:n0 + w], sy[:, :, :w], [sy.b], [db("YCG", t)], stream="st")

            class G:
                pass
            gd = []
            for d_ in range(2):
                g = G()
                g.q = [sb("gq", [128, 512], BF16, es) for _ in range(2)]
                g.k = [sb("gk", [128, 512], BF16, es) for _ in range(2)]
                g.p67 = [sb("g67", [33, 512], BF16, es) for _ in range(2)]
                g.tv = [sb("gtv", [128, 4, 384], BF16, es) for _ in range(2)]
                g.vp = [sb("gvp", [128, 4, 512], BF16, es) for _ in range(2)]
                g.so = [sb("gso", [128, 2, 512], F32, es) for _ in range(2)]
                for t_ in g.p67:
                    ms("pool", t_[32:33, :], 1.0, [t_.b])
                g.sc = []
                for par in range(2):
                    s_ = G()
                    s_.E1 = sb("E1", [128, 128], F32, es)
                    s_.SP = sb("SP", [128, 128], F32, es)
                    s_.EQ = sb("EQ", [128, 128], F32, es)
                    s_.EK = sb("EK", [128, 128], F32, es)
                    s_.QE = sb("QE", [128, 128], BF16, es)
                    s_.KE = sb("KE", [128, 4, 128], BF16, es)
                    s_.EE = sb("EE", [128, 128], F32, es)
                    s_.KEND = sb("KEND", [128, 128], BF16, es)
                    s_.DEC = sb("DEC", [128, 2], F32, es)
                    s_.AM = sb("AM", [128, 4, 128], BF16, es)
                    g.sc.append(s_)
                g.S32 = sb("S32", [128, 256], F32, es)
                g.Sb = sb("Sb", [128, 256], BF16, es)
                g.TMP = sb("STMP", [128, 256], F32, es)
                ms("pool", g.S32[:], 0.0, [g.S32.b])
                ms("pool", g.Sb[:], 0.0, [g.Sb.b])
                g.n = 0
                bkB_ = ps[3] if d_ == 0 else ps[6]
                g.bB = (TT(bkB_.t, P.buf("bBlo")), TT(bkB_.t, P.buf("bBhi")))
                gd.append(g)

            def gla_tile(d_, t, it):
                g = gd[d_]
                n0, w = TILES[t]
                nsub = w // 128
                kc0 = n0 // 128
                bi = it % 2
                q, kk_, p67, tv, vp, so = g.q[bi], g.k[bi], g.p67[bi], g.tv[bi], g.vp[bi], g.so[bi]
                P.dma("sp", q[:, :w], PGF[:, 0, n0:n0 + w], [db("PGF", t)], [q.b])
                P.dma("sp", kk_[:, :w], PGF[:, 1, n0:n0 + w], [db("PGF", t)], [kk_.b])
                P.dma("sp", p67[0:32, :w], PG67[:, n0:n0 + w], [db("PG67", t)], [p67.b])
                P.dma("sp", tv[:, 0:nsub, :], PGT[:, kc0:kc0 + nsub, :], [db("PGT", t)], [tv.b])
                P.dma("sp", vp[:, 0:nsub, :], PGVP[:, kc0:kc0 + nsub, :], [db("PGVP", t)], [vp.b])
                subs = list(range(nsub)) if d_ == 0 else list(range(nsub - 1, -1, -1))
                bkA, bkC = (ps[2], ps[4]) if d_ == 0 else (ps[5], ps[7])
                bBlo, bBhi = g.bB
                chunks = (0, 1) if d_ == 0 else (1, 0)

                def SI(j):
                    s_ = g.sc[g.n % 2]
                    g.n += 1
                    cs = slice(j * 128, (j + 1) * 128)
                    mm(bkA[:, 0:128], p67[0:33, cs], WG[l][0:33, d_, :], True, True, [p67.b, WG[l].b], [bkA.b])
                    act(s_.E1[:], bkA[:, 0:128], AF.Exp, [bkA.b], [s_.E1.b], scale=-1.0)
                    act(s_.SP[:], s_.E1[:], AF.Ln, [s_.E1.b], [s_.SP.b], bias=1.0)
                    mm(bBlo[:, 0:128], s_.SP[:], Uc[d_][:], True, True, [s_.SP.b, Uc[d_].b], [bBlo.b])
                    mm(bBlo[:, 128:256], Mend[d_][:], s_.SP[:], True, True, [s_.SP.b, Mend[d_].b], [bBlo.b])
                    mm(bkA[:, 128:130], s_.SP[:], cind[:], True, True, [s_.SP.b, cind.b], [bkA.b])
                    act(s_.EQ[:], bBlo[:, 0:128], AF.Exp, [bBlo.b], [s_.EQ.b])
                    act(s_.EK[:], bBlo[:, 0:128], AF.Exp, [bBlo.b], [s_.EK.b], scale=-1.0)
                    act(s_.EE[:], bBlo[:, 128:256], AF.Exp, [bBlo.b], [s_.EE.b])
                    act(s_.DEC[:], bkA[:, 128:130], AF.Exp, [bkA.b], [s_.DEC.b])
                    tt("dve", s_.QE[:], q[:, cs], s_.EQ[:], ALU.mult, [q.b, s_.EQ.b], [s_.QE.b])
                    for h in range(4):
                        stt("dve", s_.KE[:, h, :], kk_[:, cs], hmask[:, h:h + 1], s_.EK[:],
                            ALU.mult, ALU.mult, [kk_.b, hmask.b, s_.EK.b], [s_.KE.b])
                    tt("dve", s_.KEND[:], tv[:, j, 0:128], s_.EE[:], ALU.mult, [tv.b, s_.EE.b], [s_.KEND.b])
                    for h in range(4):
                        mm(bkA[:, h * 128:(h + 1) * 128], s_.KE[:, h, :], s_.QE[:], True, True, [s_.KE.b, s_.QE.b], [bkA.b])
                    tt("dve", s_.AM[:].rearrange("p h n -> p (h n)"), bkA[:, :],
                       mask4[d_][:].rearrange("p h n -> p (h n)"), ALU.mult, [bkA.b, mask4[d_].b], [s_.AM.b])
                    return s_

                def SD(j, s_):
                    cs = slice(j * 128, (j + 1) * 128)
                    bo = bkC
                    for oc in range(2):
                        for hh in range(2):
                            h = 2 * oc + hh
                            mm(bo[:, oc * 128:(oc + 1) * 128], vp[:, j, h * 128:(h + 1) * 128], s_.AM[:, h, :],
                               hh == 0 and oc == 0, False, [vp.b, s_.AM.b], [bo.b])
                    for ci, c in enumerate(chunks):
                        for oc in range(2):
                            mm(bo[:, oc * 128 + c * 64:oc * 128 + (c + 1) * 64], g.Sb[:, oc * 128:(oc + 1) * 128],
                               s_.QE[:, c * 64:(c + 1) * 64], False, ci == 1 and oc == 1, [g.Sb.b, s_.QE.b], [bo.b])
                        mm(bBhi[:, 256:512], s_.KEND[c * 64:(c + 1) * 64, :], tv[c * 64:(c + 1) * 64, j, 128:384], True, True,
                           [s_.KEND.b, tv.b], [bBhi.b])
                        tt("dve", g.TMP[:], bBhi[:, 256:512], bmask[:].rearrange("p h n -> p (h n)"), ALU.mult,
                           [bBhi.b, bmask.b], [g.TMP.b])
                        stt("dve", g.S32[:], g.S32[:], s_.DEC[:, c:c + 1], g.TMP[:], ALU.mult, ALU.add,
                            [g.S32.b, s_.DEC.b, g.TMP.b], [g.S32.b])
                        cp("act", g.Sb[:], g.S32[:], [g.S32.b], [g.Sb.b])
                    cp("dve", so[:, :, cs], bo[:, 0:256].rearrange("p (o n) -> p o n", o=2), [bo.b], [so.b])

                prev = None
                for j in subs:
                    yield
                    cur = (j, SI(j))
                    if prev is not None:
                        yield
                        SD(*prev)
                    prev = cur
                yield
                SD(*prev)
                P.dma("act", OFD[d_, :, :, n0:n0 + w], so[:, :, :w], [so.b], [db("OFD%d" % d_, t)], stream="st")

            order_f = list(range(9))
            order_b = [0] + list(range(8, 0, -1))
            mg = None
            if l + 1 < nlay:
                ring = [sb("wada", [128, 2304], F32, es) for _ in range(3)]
                BM["mod"] = True
                mg = mod_gen(l + 1, ring, ps[1])
            def gla_all(d_, order):
                for it in range(9):
                    yield from gla_tile(d_, order[it], it)

            def conv_all():
                for it in range(9):
                    yield from conv_tile(it)

            gens = [[gla_all(0, order_f), 3], [gla_all(1, order_b), 3], [conv_all(), 1]]
            if mg is not None:
                gens.append([mg, 1])
            live = list(gens)
            while live:
                for ge in list(live):
                    for _ in range(ge[1]):
                        try:
                            next(ge[0])
                        except StopIteration:
                            if ge in live:
                                live.remove(ge)
                            break
            BM["mod"] = False
            of = [sb("of", [128, 2, 512], F32, es) for _ in range(2)]
            ob = [sb("ob", [128, 2, 512], F32, es) for _ in range(2)]
            gr = [sb("gr", [128, 2, 512], BF16, es) for _ in range(2)]
            sg = [sb("sg", [128, 2, 512], BF16, es) for _ in range(2)]
            for t, (n0, w) in enumerate(TILES):
                a, b2_, r_, y_ = of[t % 2], ob[t % 2], gr[t % 2], sg[t % 2]
                P.dma("sp", a[:, :, :w], OFD[0, :, :, n0:n0 + w], [db("OFD0", t)], [a.b])
                P.dma("sp", b2_[:, :, :w], OFD[1, :, :, n0:n0 + w], [db("OFD1", t)], [b2_.b])
                P.dma("sp", r_[:, :, :w], PGF[:, 2:4, n0:n0 + w], [db("PGF", t)], [r_.b])
                tt("dve", a[:, :, :w], a[:, :, :w], b2_[:, :, :w], ALU.add, [a.b, b2_.b], [a.b])
                for oc in range(2):
                    sq = ft()
                    tt("pool", sq[:, :w], a[:, oc, :w], a[:, oc, :w], ALU.mult, [a.b], [sq.b])
                    b1 = bank("st")
                    mm(b1[:, :w], blkf[:], sq[:, :w], True, True, [blkf.b, sq.b], [b1.b])
                    rs = ft()
                    rsqrt_act(rs[:, :w], b1[:, :w], [b1.b, epsT.b], [rs.b])
                    o2 = ft()
                    stt("dve", o2[:, :w], a[:, oc, :w], pa[:, 104 + oc:105 + oc], rs[:, :w], ALU.mult, ALU.mult,
                        [a.b, pa.b, rs.b], [o2.b])
                    tt("dve", y_[:, oc, :w], o2[:, :w], r_[:, oc, :w], ALU.mult, [o2.b, r_.b], [y_.b])
                P.dma("act", YCG[:, 2:4, n0:n0 + w], y_[:, :, :w], [y_.b], [db("YCG", t)], stream="st")
            P.barrier()

    sweep(None, 0)
    if stop == "pre":
        return finish(nc, P)
    for l in range(nlay):
        phase2a(l)
        if stop == "p2a%d" % l:
            return finish(nc, P)
        sweep(l, l + 1 if l + 1 < nlay else None)
        if stop == "sw%d" % l:
            return finish(nc, P)
    return finish(nc, P)


def finish(nc, P):
    P.barrier()
    P.emit()
    return nc, P


_CACHE = {}


def make_in_maps(inputs):
    f = lambda a: np.ascontiguousarray(np.asarray(a, dtype=np.float32))
    shared = {
        "c_ctx": f(inputs["c_ctx"]).reshape(8, 128),
        "w_ada": f(inputs["w_ada"]),
        "b_ada": f(inputs["b_ada"]).reshape(DEPTH, 72, 128),
        "g_norm": f(inputs["g_norm"]).reshape(DEPTH, 24, 128),
        "w_ffn_in": f(inputs["w_ffn_in"]),
        "w_ffn_out": f(inputs["w_ffn_out"]),
        "w_in": f(inputs["w_in"]),
        "w_out": f(inputs["w_out"]),
        "w_dw": f(inputs["w_dw"]).reshape(DEPTH, 62, 128),
        "b_dw": f(inputs["b_dw"]).reshape(DEPTH, 2, 128),
        "conv_norm_g": f(inputs["conv_norm_g"]).reshape(DEPTH, 2, 128),
        "conv_norm_b": f(inputs["conv_norm_b"]).reshape(DEPTH, 2, 128),
        "w_gla_gate": f(inputs["w_gla_gate"]),
        "b_gla_gate": f(inputs["b_gla_gate"]),
        "gla_norm_g": f(inputs["gla_norm_g"]).reshape(DEPTH, 2, 128),
        "q_norm_g": f(inputs["q_norm_g"]).reshape(DEPTH, 1, 64),
        "k_norm_g": f(inputs["k_norm_g"]).reshape(DEPTH, 1, 64),
    }
    x = f(inputs["x"])
    c = f(inputs["c"])
    ctx = f(inputs["ctx"])
    maps = []
    for b in range(8):
        m = dict(shared)
        m["x"] = x[b]
        m["c"] = c[b].reshape(8, 128)
        m["ctx"] = ctx[b]
        maps.append(m)
    return maps


def kernel(**inputs):
    if "nc" not in _CACHE:
        _CACHE["nc"] = build()[0]
    nc = _CACHE["nc"]
    res = run_bass_kernel_spmd(nc, make_in_maps(inputs), core_ids=list(range(8)))
    return np.stack([np.asarray(r["out"], dtype=np.float32) for r in res.results], axis=0)
```
